# Optimizing a Trainium2 kernel written in Bass

```python
import jax, jax.numpy as jnp
from jax import lax
import numpy as np

D_MODEL = 1024
BATCH = 32
SEQ = 2048
DEPTH = 4

CHUNK = 64
N_META = 16
Q_BLOCK = 128
PAD = Q_BLOCK - N_META
N_A_LAYERS = DEPTH // 2
N_B_LAYERS = DEPTH - N_A_LAYERS
RMS_EPS = 1e-6
GN_EPS = 1e-5
ROPE_THETA = 10000.0
MASK_VALUE = -1e30

RET_HEADS = 4
RET_QK_DIM = D_MODEL // RET_HEADS
RET_V_DIM = 2 * D_MODEL // RET_HEADS
RET_IN = 2 * RET_HEADS * RET_QK_DIM + 2 * RET_HEADS * RET_V_DIM

MLA_HEADS = 8
MLA_NOPE = 128
MLA_ROPE = 64
MLA_V = 128
MLA_Q_RANK = 768
MLA_KV_RANK = 512

FFN_HIDDEN = ((8 * D_MODEL + 3 * 256 - 1) // (3 * 256)) * 256

kernel_name = "yoco_retention_mla_meta_chunk_causal_trunk"


def rmsnorm(x, g):
    xf = x.astype(jnp.float32)
    y = xf * lax.rsqrt(jnp.mean(xf * xf, axis=-1, keepdims=True) + RMS_EPS)
    return (y * g.astype(jnp.float32)).astype(x.dtype)


def rope_tables(pos, dim):
    inv = 1.0 / (ROPE_THETA ** (jnp.arange(0, dim, 2, dtype=jnp.float32) / dim))
    ang = pos.astype(jnp.float32)[:, None] * inv[None, :]
    return jnp.cos(ang), jnp.sin(ang)


def apply_rope(x, cos, sin):
    xf = x.astype(jnp.float32)
    half = x.shape[-1] // 2
    x1, x2 = xf[..., :half], xf[..., half:]
    c = cos[None, :, None, :]
    s = sin[None, :, None, :]
    return jnp.concatenate([x1 * c - x2 * s, x1 * s + x2 * c], axis=-1).astype(x.dtype)


def retention_mixer(hn, w_in, gn_g, w_o, cos, sin, valid):
    B, L, _ = hn.shape
    H, DK, DV, C = RET_HEADS, RET_QK_DIM, RET_V_DIM, CHUNK
    NC = L // C
    dt = hn.dtype
    proj = hn @ w_in
    q = proj[..., :H * DK].reshape(B, L, H, DK)
    k = proj[..., H * DK:2 * H * DK].reshape(B, L, H, DK)
    v = proj[..., 2 * H * DK:2 * H * DK + H * DV].reshape(B, L, H, DV)
    gate = proj[..., 2 * H * DK + H * DV:]
    q = apply_rope(q, cos, sin)
    k = apply_rope(k, cos, sin) * (DK ** -0.5) * valid[None, :, None, None]

    def to_chunks(t):
        return t.reshape(B, NC, C, H, t.shape[-1]).transpose(1, 0, 3, 2, 4)

    log_g = jnp.log1p(-jnp.exp2(-5.0 - jnp.arange(H, dtype=jnp.float32)))
    idx = jnp.arange(C, dtype=jnp.float32)
    intra = jnp.exp(log_g[:, None, None] * jnp.abs(idx[:, None] - idx[None, :])).astype(dt)
    q_dec = jnp.exp(log_g[:, None] * (idx + 1.0)).astype(dt)
    k_dec = jnp.exp(log_g[:, None] * (C - 1.0 - idx)).astype(dt)
    chunk_dec = jnp.exp(log_g * C).astype(dt)

    def step(S, qkv):
        qc, kc, vc = qkv
        sc = jnp.einsum('bhid,bhjd->bhij', qc, kc) * intra
        o = (jnp.einsum('bhij,bhjv->bhiv', sc, vc)
             + jnp.einsum('bhid,bhdv->bhiv', qc * q_dec[:, :, None], S))
        S = S * chunk_dec[:, None, None] + jnp.einsum('bhjd,bhjv->bhdv', kc * k_dec[:, :, None], vc)
        return S, o

    S0 = jnp.zeros((B, H, DK, DV), dt)
    _, o = lax.scan(step, S0, (to_chunks(q), to_chunks(k), to_chunks(v)))
    o = o.transpose(1, 0, 3, 2, 4).reshape(B, L, H, DV)
    of = o.astype(jnp.float32)
    mu = jnp.mean(of, axis=-1, keepdims=True)
    var = jnp.mean(jnp.square(of - mu), axis=-1, keepdims=True)
    on = ((of - mu) * lax.rsqrt(var + GN_EPS)).reshape(B, L, H * DV) * gn_g.astype(jnp.float32)
    return (jax.nn.silu(gate) * on.astype(dt)) @ w_o


def mla_shared_kv(h, norm_kv_g, w_kv_a, kv_a_norm_g, w_kv_b, cos, sin):
    B, L, _ = h.shape
    kv_a = rmsnorm(h, norm_kv_g) @ w_kv_a
    c_kv = rmsnorm(kv_a[..., :MLA_KV_RANK], kv_a_norm_g)
    k_rope = apply_rope(kv_a[..., MLA_KV_RANK:][:, :, None, :], cos, sin)
    kv = (c_kv @ w_kv_b).reshape(B, L, MLA_HEADS, MLA_NOPE + MLA_V)
    k = jnp.concatenate([kv[..., :MLA_NOPE],
                         jnp.broadcast_to(k_rope, (B, L, MLA_HEADS, MLA_ROPE))], axis=-1)
    return k, kv[..., MLA_NOPE:]


def mla_mixer(hn, w_q_a, q_a_norm_g, w_q_b, w_o, k, v, cos, sin, chunk_id, valid_key):
    B, L, _ = hn.shape
    cq = rmsnorm(hn @ w_q_a, q_a_norm_g)
    q = (cq @ w_q_b).reshape(B, L, MLA_HEADS, MLA_NOPE + MLA_ROPE)
    q = jnp.concatenate([q[..., :MLA_NOPE], apply_rope(q[..., MLA_NOPE:], cos, sin)], axis=-1)
    q = q * ((MLA_NOPE + MLA_ROPE) ** -0.5)
    outs = []
    for qb in range(L // Q_BLOCK):
        s, e = qb * Q_BLOCK, (qb + 1) * Q_BLOCK
        sc = jnp.einsum('bqhd,bkhd->bhqk', q[:, s:e], k[:, :e]).astype(jnp.float32)
        mask = (chunk_id[None, :e] <= chunk_id[s:e, None]) & valid_key[None, :e]
        sc = jnp.where(mask[None, None], sc, MASK_VALUE)
        p = jax.nn.softmax(sc, axis=-1).astype(v.dtype)
        outs.append(jnp.einsum('bhqk,bkhd->bqhd', p, v[:, :e]))
    o = jnp.concatenate(outs, axis=1).reshape(B, L, MLA_HEADS * MLA_V)
    return o @ w_o


def swiglu(hn, w1, w3, w2):
    return (jax.nn.silu(hn @ w1) * (hn @ w3)) @ w2


def setup_inputs(seed: int = 0) -> dict:
    key = jax.random.key(seed)
    ks = jax.random.split(key, 24)
    f32 = jnp.float32
    res = (2 * DEPTH) ** -0.5

    def w(k, shape, fan_in, scale=1.0):
        return jax.random.normal(k, shape, f32) * (fan_in ** -0.5) * scale

    def gain(k, shape):
        return 1.0 + 0.02 * jax.random.normal(k, shape, f32)

    return {
        "x": jax.random.normal(ks[0], (BATCH, SEQ, D_MODEL), f32),
        "meta": jax.random.normal(ks[1], (N_META, D_MODEL), f32),
        "norm_mix_g": gain(ks[2], (DEPTH, D_MODEL)),
        "norm_ffn_g": gain(ks[3], (DEPTH, D_MODEL)),
        "ret_w_in": w(ks[4], (N_A_LAYERS, D_MODEL, RET_IN), D_MODEL),
        "ret_gn_g": gain(ks[5], (N_A_LAYERS, RET_HEADS * RET_V_DIM)),
        "ret_w_o": w(ks[6], (N_A_LAYERS, RET_HEADS * RET_V_DIM, D_MODEL), RET_HEADS * RET_V_DIM, res),
        "mla_norm_kv_g": gain(ks[7], (D_MODEL,)),
        "mla_w_kv_a": w(ks[8], (D_MODEL, MLA_KV_RANK + MLA_ROPE), D_MODEL),
        "mla_kv_a_norm_g": gain(ks[9], (MLA_KV_RANK,)),
        "mla_w_kv_b": w(ks[10], (MLA_KV_RANK, MLA_HEADS * (MLA_NOPE + MLA_V)), MLA_KV_RANK),
        "mla_w_q_a": w(ks[11], (N_B_LAYERS, D_MODEL, MLA_Q_RANK), D_MODEL),
        "mla_q_a_norm_g": gain(ks[12], (N_B_LAYERS, MLA_Q_RANK)),
        "mla_w_q_b": w(ks[13], (N_B_LAYERS, MLA_Q_RANK, MLA_HEADS * (MLA_NOPE + MLA_ROPE)), MLA_Q_RANK),
        "mla_w_o": w(ks[14], (N_B_LAYERS, MLA_HEADS * MLA_V, D_MODEL), MLA_HEADS * MLA_V, res),
        "ffn_w1": w(ks[15], (DEPTH, D_MODEL, FFN_HIDDEN), D_MODEL),
        "ffn_w3": w(ks[16], (DEPTH, D_MODEL, FFN_HIDDEN), D_MODEL),
        "ffn_w2": w(ks[17], (DEPTH, FFN_HIDDEN, D_MODEL), FFN_HIDDEN, res),
        "final_g": gain(ks[18], (D_MODEL,)),
    }


def reference(x, meta, norm_mix_g, norm_ffn_g, ret_w_in, ret_gn_g, ret_w_o,
              mla_norm_kv_g, mla_w_kv_a, mla_kv_a_norm_g, mla_w_kv_b,
              mla_w_q_a, mla_q_a_norm_g, mla_w_q_b, mla_w_o,
              ffn_w1, ffn_w3, ffn_w2, final_g):
    B, S, D = x.shape
    L = PAD + N_META + S
    h = jnp.concatenate([jnp.zeros((B, PAD, D), x.dtype),
                         jnp.broadcast_to(meta[None].astype(x.dtype), (B, N_META, D)),
                         x], axis=1)
    slot = jnp.arange(L)
    chunk_id = slot // CHUNK
    valid_key = slot >= PAD
    valid = valid_key.astype(x.dtype)
    pos = slot - PAD
    cos_r, sin_r = rope_tables(pos, RET_QK_DIM)
    cos_m, sin_m = rope_tables(pos, MLA_ROPE)

    k_sh = None
    v_sh = None
    for layer in range(DEPTH):
        hn = rmsnorm(h, norm_mix_g[layer])
        if layer < N_A_LAYERS:
            h = h + retention_mixer(hn, ret_w_in[layer], ret_gn_g[layer], ret_w_o[layer],
                                    cos_r, sin_r, valid)
        else:
            if layer == N_A_LAYERS:
                k_sh, v_sh = mla_shared_kv(h, mla_norm_kv_g, mla_w_kv_a, mla_kv_a_norm_g,
                                           mla_w_kv_b, cos_m, sin_m)
            j = layer - N_A_LAYERS
            h = h + mla_mixer(hn, mla_w_q_a[j], mla_q_a_norm_g[j], mla_w_q_b[j], mla_w_o[j],
                              k_sh, v_sh, cos_m, sin_m, chunk_id, valid_key)
        h = h + swiglu(rmsnorm(h, norm_ffn_g[layer]), ffn_w1[layer], ffn_w3[layer], ffn_w2[layer])
    return rmsnorm(h, final_g)[:, PAD + N_META:]
```

```python
import numpy as np
import concourse.bass as bass
import concourse.mybir as mybir
from concourse.bass_utils import run_bass_kernel_spmd
from contextlib import ExitStack

F32 = mybir.dt.float32
BF16 = mybir.dt.bfloat16
AF = mybir.ActivationFunctionType
ALU = mybir.AluOpType

D = 1024
SEQ = 2048
PAD = 112
L = 2176
NB = 17
NCORES = 8
BATCH = 32
HID = 2816
RMS_EPS = 1e-6
GN_EPS = 1e-5
TGS = [(0, 4), (4, 4), (8, 4), (12, 4), (16, 1)]
NRING = 5

G_MIX = 0
G_FFN = 32
G_KV = 64
G_GN = 72
G_QA = 104
G_KVA = 116
G_TOT = 120

ENGS = ("pe", "act", "dve", "pool", "sp")


class Tile:
    __slots__ = ("name", "w", "rd")

    def __init__(self, name):
        self.name = name
        self.w = None
        self.rd = []


class Op:
    __slots__ = ("eng", "fn", "deps", "sig", "cnt", "dma", "dsem", "dcnt", "idx", "waits")

    def __init__(self, eng, fn):
        self.eng = eng
        self.fn = fn
        self.deps = []
        self.sig = False
        self.cnt = 0
        self.dma = 0
        self.dsem = None
        self.dcnt = 0
        self.waits = None


class Sched:
    def __init__(self, nc):
        self.nc = nc
        self.ops = []
        self.dma_sems = {}

    def op(self, eng, fn, reads=(), writes=(), dma=0, dsem=None):
        o = Op(eng, fn)
        o.dma = dma
        deps = {}
        for t in reads:
            if t.w is not None:
                deps[id(t.w)] = t.w
        for t in writes:
            if t.w is not None:
                deps[id(t.w)] = t.w
            for r in t.rd:
                deps[id(r)] = r
        for d in deps.values():
            if d is o:
                continue
            if d.dma == 0 and d.eng == eng and eng == "pe" and dma == 0:
                continue
            o.deps.append(d)
        for t in reads:
            t.rd.append(o)
        for t in writes:
            t.w = o
            t.rd = []
        if dma:
            self.dma_sems[dsem] = self.dma_sems.get(dsem, 0) + dma
            o.dsem = dsem
            o.dcnt = self.dma_sems[dsem]
        o.idx = len(self.ops)
        self.ops.append(o)
        return o

    def finalize(self):
        for o in self.ops:
            for d in o.deps:
                if d.dma == 0:
                    d.sig = True
        cnt = {e: 0 for e in ENGS}
        for o in self.ops:
            if o.dma == 0 and o.sig:
                cnt[o.eng] += 1
                o.cnt = cnt[o.eng]
        ei = {e: i for i, e in enumerate(ENGS)}
        ne = len(ENGS)
        known = {e: [0] * ne for e in ENGS}
        known_d = {e: {} for e in ENGS}
        clocks = [None] * len(self.ops)
        for o in self.ops:
            k = known[o.eng]
            kd = known_d[o.eng]
            best = {}
            for d in o.deps:
                if d.dma:
                    need = d.dcnt * 16
                    if kd.get(d.dsem, 0) < need:
                        kd[d.dsem] = need
                        best[("d", d.dsem)] = need
                    dc = clocks[d.idx]
                    for i in range(ne):
                        if dc[i] > k[i]:
                            k[i] = dc[i]
                else:
                    j = ei[d.eng]
                    if k[j] < d.cnt:
                        key = ("e", d.eng)
                        if best.get(key, 0) < d.cnt:
                            best[key] = d.cnt
                        dc = clocks[d.idx]
                        for i in range(ne):
                            if dc[i] > k[i]:
                                k[i] = dc[i]
                        if k[j] < d.cnt:
                            k[j] = d.cnt
            o.waits = [(a, b, v) for (a, b), v in best.items()]
            c = list(k)
            if o.dma == 0 and o.sig:
                c[ei[o.eng]] = max(c[ei[o.eng]], o.cnt)
            clocks[o.idx] = c
        return cnt

    def emit(self, final_waits=()):
        nc = self.nc
        self.finalize()
        with ExitStack() as es:
            esem = {e: es.enter_context(nc.semaphore("s_" + e)) for e in ENGS}
            dsem = {k: es.enter_context(nc.semaphore("d_%s" % (k,))) for k in self.dma_sems}
            block = es.enter_context(nc.Block())
            per = {e: [o for o in self.ops if o.eng == e] for e in ENGS}

            def run(eng_name, eng):
                for o in per[eng_name]:
                    for (kind, key, val) in o.waits:
                        if kind == "e":
                            eng.wait_ge(esem[key], val)
                        else:
                            eng.wait_ge(dsem[key], val)
                    if o.dma:
                        o.fn(eng, dsem[o.dsem])
                    else:
                        ins = o.fn(eng)
                        if o.sig:
                            ins.then_inc(esem[o.eng], 1)
                if eng_name == "sp":
                    for k in final_waits:
                        eng.wait_ge(dsem[k], self.dma_sems[k] * 16)

            @block.tensor
            def _(e):
                run("pe", e)

            @block.scalar
            def _(e):
                run("act", e)

            @block.vector
            def _(e):
                run("dve", e)

            @block.gpsimd
            def _(e):
                run("pool", e)

            @block.sync
            def _(e):
                run("sp", e)


class RPool:
    def __init__(self, items):
        self.items = items
        self.i = 0

    def get(self):
        it = self.items[self.i % len(self.items)]
        self.i += 1
        return it


def _consts():
    f32 = np.float32
    slot = np.arange(L)
    pos = (slot - PAD).astype(f32)

    def tables(dim):
        inv = (1.0 / (f32(10000.0) ** (np.arange(0, dim, 2, dtype=f32) / f32(dim)))).astype(f32)
        ang = (pos[:, None] * inv[None, :]).astype(f32)
        return np.cos(ang).astype(f32), np.sin(ang).astype(f32)

    cr, sr = tables(256)
    cm, sm = tables(64)
    c = {}
    c["ret_cosT"] = np.ascontiguousarray(cr.T)
    c["ret_sinT"] = np.ascontiguousarray(sr.T)
    sc = f32(192.0 ** -0.5)
    c["mla_CS"] = np.ascontiguousarray(np.concatenate([cm.T, cm.T, -sm.T, sm.T], 0) * sc).astype(f32)
    kt = np.concatenate([cm, sm], 1).reshape(NB, 128, 64).transpose(1, 0, 2)
    c["mla_ktab"] = np.ascontiguousarray(kt).astype(f32)
    c["ident"] = np.eye(128, dtype=f32)
    c["ones"] = np.ones((128, 128), f32)
    v0 = np.ones((128, 128), f32)
    v0[:PAD, :] = 1e-30
    c["valid0"] = v0
    H = 4
    log_g = np.log1p(-np.exp2(-5.0 - np.arange(H, dtype=np.float64)))
    i = np.arange(128, dtype=np.float64)
    MTp = np.zeros((128, H, 128), np.float64)
    cc = np.zeros((128, 16), np.float64)
    gam128 = []
    for h in range(H):
        lg = log_g[h]
        qdec = np.exp(lg * (i + 1.0))
        ii, jj = np.meshgrid(i, i, indexing="ij")
        same = (ii // 64) == (jj // 64)
        M = np.where(same, np.exp(lg * np.abs(ii - jj)),
                     np.where(ii > jj, np.exp(lg * (ii - jj)), 0.0))
        Mp = M / qdec[:, None] / 16.0
        MTp[:, h, :] = Mp.T
        cc[:, h] = np.exp(lg * (127.0 - i)) / 16.0
        cc[:, 4 + h] = GN_EPS / (qdec ** 2)
        gam128.append(float(np.exp(lg * 128.0)))
    cc[:, 8] = -0.5
    c["MTp"] = MTp.astype(f32)
    c["ccols"] = cc.astype(f32)
    return c, gam128


def _fm(g, k):
    return np.ascontiguousarray(np.asarray(g, np.float32).reshape(k, 128).T)


def build_nc(nseq=4, stop_after=9, gam128=None):
    nc = bass.Bass("TRN2", target_bir_lowering=False)

    def din(name, shape):
        return nc.dram_tensor(name, list(shape), F32, kind="ExternalInput").ap()

    x = din("x", [nseq, SEQ, D])
    meta = din("meta", [16, D])
    ret_w_in = din("ret_w_in", [2, D, 6144])
    ret_w_o = din("ret_w_o", [2, 2048, D])
    w_kv_a = din("mla_w_kv_a", [D, 576])
    w_kv_b = din("mla_w_kv_b", [512, 2048])
    w_q_a = din("mla_w_q_a", [2, D, 768])
    w_q_b = din("mla_w_q_b", [2, 768, 1536])
    mla_w_o = din("mla_w_o", [2, D, D])
    ffn_w1 = din("ffn_w1", [4, D, HID])
    ffn_w3 = din("ffn_w3", [4, D, HID])
    ffn_w2 = din("ffn_w2", [4, HID, D])
    final_g = din("final_g", [1, D])
    gfm_d = din("gfm", [128, G_TOT])
    ret_cosT = din("ret_cosT", [128, L])
    ret_sinT = din("ret_sinT", [128, L])
    mla_CS = din("mla_CS", [128, L])
    mla_ktab = din("mla_ktab", [128, NB * 64])
    ident_d = din("ident", [128, 128])
    ones_d = din("ones", [128, 128])
    valid0_d = din("valid0", [128, 128])
    MTp_d = din("MTp", [128, 4 * 128])
    ccols_d = din("ccols", [128, 16])
    out = nc.dram_tensor("out", [nseq, SEQ, D], F32, kind="ExternalOutput").ap()

    es = ExitStack()

    def sb(name, shape, dt):
        return es.enter_context(nc.sbuf_tensor(name, shape, dt))

    S = Sched(nc)

    h = sb("h_sb", [128, NB, D], F32)
    hT = [Tile("h%d" % t) for t in range(NB)]
    hnT = sb("hnT", [128, 8, L], BF16)
    hnTT = [Tile("hnT%d" % t) for t in range(NB)]
    ring = sb("ring", [128, NRING, 4096], BF16)
    ckvT = sb("ckvT", [128, 4, L], BF16)
    ckvTT = [Tile("ckv%d" % t) for t in range(NB)]
    ckv_flat = ckvT[:, :, :].rearrange("p k n -> p (k n)")
    ringT = [[Tile("ring%d" % s)] for s in range(NRING)] + [ckvTT, ckvTT]
    nring = [7]

    def rv(s):
        if s < NRING:
            return ring[:, s, :]
        return ckv_flat[:, (s - NRING) * 4096:(s - NRING + 1) * 4096]
    kropeT = sb("kropeT", [128, L], BF16)
    krTT = [Tile("kr%d" % t) for t in range(NB)]
    longs = [sb("long%d" % i, [128, L], BF16) for i in range(2)]
    lqT = [Tile("lq%d" % i) for i in range(2)]
    lkT = [Tile("lk%d" % i) for i in range(2)]
    Fb = [sb("f%d" % i, [128, 512], F32) for i in range(6)]
    FT = [Tile("f%d" % i) for i in range(6)]
    Bb = [sb("b%d" % i, [128, 512], BF16) for i in range(8)]
    BT = [Tile("b%d" % i) for i in range(8)]
    junk = sb("junk", [128, 1024], BF16)
    junkT = Tile("junk")
    Fj = junk.bitcast(F32)
    hnb = [sb("hnb%d" % i, [128, 1024], BF16) for i in range(2)]
    hnbT = [Tile("hnb%d" % i) for i in range(2)]
    Sst = sb("Sst", [128, 2, 512], F32)
    SstT = Tile("Sst")
    Sbf = sb("Sbf", [128, 2, 512], BF16)
    SbfT = Tile("Sbf")
    NST = 12
    stt_ = sb("stats", [128, NST, 8], F32)
    stT = [Tile("st%d" % i) for i in range(NST)]
    ssq = sb("ssq", [128, 2 * NB], F32)
    ssqT = [Tile("ssq%d" % i) for i in range(2 * NB)]
    ident = sb("identb", [128, 128], BF16)
    ones = sb("onesb", [128, 128], BF16)
    valid0 = sb("valid0b", [128, 128], BF16)
    MTp = sb("MTp_sb", [128, 4, 128], F32)
    ccols = sb("ccols_sb", [128, 16], F32)
    gfm = sb("gfm_sb", [128, G_TOT], F32)
    constT = Tile("consts")
    ps = es.enter_context(nc.psum_tensor("ps", [128, 8, 512], F32))
    psb = ps.bitcast(BF16)
    pT = [Tile("ps%d" % i) for i in range(8)]
    pT0a, pT0b, pT0c = Tile("ps0a"), Tile("ps0b"), Tile("ps0c")

    PA = RPool([0, 1])
    PA3 = RPool([0, 1, 7])
    PS3 = RPool([2, 3, 6])
    PB = RPool([2, 3])
    PC = RPool([4, 5])
    PD = RPool([6, 7])
    PDD = RPool([(4, 5), (6, 7)])
    Ball = RPool(list(range(8)))
    Brot = RPool([4, 5, 6, 7])
    hnbH = [Tile("hnbh%d" % i) for i in range(4)]
    PTB = RPool([(Bb[4], BT[4]), (hnb[0][:, 0:512], hnbH[0]), (Bb[5], BT[5]), (hnb[0][:, 512:1024], hnbH[1]),
                 (hnb[1][:, 0:512], hnbH[2]), (hnb[1][:, 512:1024], hnbH[3])])

    def alias_in(news, olds):
        for n_ in news:
            n_.w = None
            n_.rd = []
            for o_ in olds:
                n_.rd = n_.rd + list(o_.rd) + ([o_.w] if o_.w is not None else [])

    def alias_out(news, olds):
        for o_ in olds:
            for n_ in news:
                o_.rd = o_.rd + list(n_.rd) + ([n_.w] if n_.w is not None else [])

    Frot = RPool([2, 3, 4, 5])
    HNB = RPool([0, 1])
    ST = RPool(list(range(NST)))
    ring_ctr = [0]

    def ring_alloc():
        s = ring_ctr[0] % nring[0]
        ring_ctr[0] += 1
        return s

    def blk(t):
        return slice(t * 128, (t + 1) * 128)

    def mm(o, lhsT, rhs, start, stop, reads, writes):
        S.op("pe", lambda e: e.matmul(o, lhsT=lhsT, rhs=rhs, start=start, stop=stop), reads, writes)

    def tr(o, in_, reads, writes):
        S.op("pe", lambda e: e.transpose(out=o, in_=in_, identity=ident[:]), list(reads) + [constT], writes)

    def act(o, in_, func, reads, writes, scale=None, bias=None, accum=None):
        kw = {}
        if scale is not None:
            kw["scale"] = scale
        if bias is not None:
            kw["bias"] = bias
        if accum is not None:
            kw["accum_out"] = accum
        S.op("act", lambda e: e.activation(out=o, in_=in_, func=func, **kw), reads, writes)

    def tt(eng, o, a, b, op, reads, writes):
        S.op(eng, lambda e: e.tensor_tensor(out=o, in0=a, in1=b, op=op), reads, writes)

    def ts(eng, o, a, s1, s2, op0, op1, reads, writes):
        if s2 is None:
            S.op(eng, lambda e: e.tensor_scalar(out=o, in0=a, scalar1=s1, scalar2=None, op0=op0), reads, writes)
        else:
            S.op(eng, lambda e: e.tensor_scalar(out=o, in0=a, scalar1=s1, scalar2=s2, op0=op0, op1=op1), reads, writes)

    def stt(eng, o, a, sc, b, op0, op1, reads, writes):
        S.op(eng, lambda e: e.scalar_tensor_tensor(out=o, in0=a, scalar=sc, in1=b, op0=op0, op1=op1), reads, writes)

    def dma(eng, pairs, reads, writes, dsem):
        def fn(e, s):
            for (o, i) in pairs:
                e.dma_start(out=o, in_=i).then_inc(s, 16)
        S.op(eng, fn, reads, writes, dma=len(pairs), dsem=dsem)

    def rsqrt_col(o, in_, reads, writes):
        tt("pool", o, in_, ccols[:, 8:9], ALU.pow, list(reads) + [constT], writes)

    dma("pool", [(ident[:], ident_d), (ones[:], ones_d), (valid0[:], valid0_d)], [], [constT], "c0")
    dma("sp", [(MTp[:], MTp_d.rearrange("p (h i) -> p h i", h=4)), (ccols[:], ccols_d), (gfm[:], gfm_d)], [], [constT], "c1")

    def seq_begin(b):
        S.op("pool", lambda e: e.memset(h[:, 0, :], 0.0), [], [hT[0]])
        dma("sp", [(h[PAD:128, 0, :], meta)], [], [hT[0]], "hin4")
        for c in range(4):
            dma("sp", [(h[:, 1 + 4 * c:5 + 4 * c, :],
                        x[b, c * 512:(c + 1) * 512, :].rearrange("(t p) d -> p t d", p=128))],
                [], [hT[1 + 4 * c + i] for i in range(4)], "hin%d" % c)

    def norm_phase(gcol):
        banks = {}

        def s1(t):
            act(junk[:], h[:, t, :], AF.Square, [hT[t]], [ssqT[t], junkT], accum=ssq[:, t:t + 1])
            ts("dve", ssq[:, t:t + 1], ssq[:, t:t + 1], 1.0 / D, RMS_EPS, ALU.mult, ALU.add, [ssqT[t]], [ssqT[t]])
            rsqrt_col(ssq[:, NB + t:NB + t + 1], ssq[:, t:t + 1], [ssqT[t]], [ssqT[NB + t]])

        def s2(t):
            hbi = HNB.get()
            hb = hnb[hbi]
            act(hb[:], h[:, t, :], AF.Identity, [hT[t], ssqT[NB + t]], [hnbT[hbi]], scale=ssq[:, NB + t:NB + t + 1])
            bk = PA.get()
            pv = psb[:, bk, :].rearrange("p (k n) -> p k n", k=8)
            for k in range(8):
                tr(pv[:, k, :], hb[:, k * 128:(k + 1) * 128], [hnbT[hbi]], [pT[bk]])
            banks[t] = (bk, pv)

        def s3(t):
            bk, pv = banks[t]
            tt("dve", hnT[:, :, blk(t)], pv, gfm[:, gcol:gcol + 8].unsqueeze(2).broadcast_to([128, 8, 128]),
               ALU.mult, [pT[bk], constT], [hnTT[t]])

        s1(0)
        s1(1)
        for t in range(NB):
            if t + 2 < NB:
                s1(t + 2)
            s2(t)
            if t > 0:
                s3(t - 1)
        s3(NB - 1)

    def ffn(l, norm=None):
        npair = HID // 256
        look = max(1, nring[0] // 2 - 1)

        def load(p):
            sA = ring_alloc()
            sB = ring_alloc()
            w1v = rv(sA)[:, 0:2048].rearrange("p (k n) -> p k n", k=8)
            w3v = rv(sA)[:, 2048:4096].rearrange("p (k n) -> p k n", k=8)
            w2v = rv(sB)[:, 0:2048].rearrange("p (c n) -> p c n", c=2)
            dma("pool", [(w1v, ffn_w1[l, :, p * 256:(p + 1) * 256].rearrange("(k p) n -> p k n", p=128)),
                         (w3v, ffn_w3[l, :, p * 256:(p + 1) * 256].rearrange("(k p) n -> p k n", p=128))],
                [], ringT[sA], "ring%d" % sA)
            dma("pool", [(w2v, ffn_w2[l, p * 256:(p + 1) * 256, :].rearrange("(c p) n -> p c n", p=128))],
                [], ringT[sB], "ring%d" % sB)
            return (sA, sB, w1v, w3v, w2v)

        def compute(u):
            sA, sB, w1v, w3v, w2v = u
            pend = None

            def down(b0, nb, ajs):
                for tb in range(nb):
                    t = b0 + tb
                    dd = PDD.get()
                    for half in range(2):
                        for j in range(2):
                            mm(ps[:, dd[half], :], Bb[ajs[j]][:, blk(tb)], w2v[:, j, half * 512:(half + 1) * 512],
                               j == 0, j == 1, [BT[ajs[j]]] + ringT[sB], [pT[dd[half]]])
                    tt("dve", h[:, t, :].rearrange("p (a n) -> p a n", a=2),
                       h[:, t, :].rearrange("p (a n) -> p a n", a=2), ps[:, dd[0]:dd[0] + 2, :], ALU.add,
                       [hT[t], pT[dd[0]], pT[dd[1]]], [hT[t]])

            for (b0, nb) in TGS:
                N = nb * 128
                c0 = PAD if b0 == 0 else 0
                cols = slice(b0 * 128 + c0, b0 * 128 + N)
                hr = [hnTT[b0 + i] for i in range(nb)]
                ajs = []
                for j in range(2):
                    bg = PA.get()
                    bu = PB.get()
                    for k in range(8):
                        mm(ps[:, bg, c0:N], w1v[:, k, j * 128:(j + 1) * 128], hnT[:, k, cols], k == 0, k == 7,
                           ringT[sA] + hr, [pT[bg]])
                    for k in range(8):
                        mm(ps[:, bu, c0:N], w3v[:, k, j * 128:(j + 1) * 128], hnT[:, k, cols], k == 0, k == 7,
                           ringT[sA] + hr, [pT[bu]])
                    fi = Frot.get()
                    act(Fb[fi][:, c0:N], ps[:, bg, c0:N], AF.Silu, [pT[bg]], [FT[fi]])
                    bi = Ball.get()
                    if c0:
                        S.op("pool", lambda e, bi=bi, c0=c0: e.memset(Bb[bi][:, 0:c0], 0.0), [], [BT[bi]])
                    tt("dve", Bb[bi][:, c0:N], ps[:, bu, c0:N], Fb[fi][:, c0:N], ALU.mult, [pT[bu], FT[fi]], [BT[bi]])
                    ajs.append(bi)
                if pend is not None:
                    down(*pend)
                pend = (b0, nb, ajs)
            down(*pend)

        us = {}
        for p in range(min(look, npair)):
            us[p] = load(p)
        if norm is not None:
            norm()
        for p in range(npair):
            if p + look < npair:
                us[p + look] = load(p + look)
            compute(us.pop(p))

    def retention(l, norm=None):
        gam = gam128
        scT_ = kdT_ = ytT_ = pT[0]

        def load(hd, which):
            res = {}
            if "qk" in which:
                s = ring_alloc()
                v = rv(s).rearrange("p (k n) -> p k n", k=8)
                dma("pool", [(v[:, :, 0:256], ret_w_in[l, :, hd * 256:(hd + 1) * 256].rearrange("(k p) n -> p k n", p=128)),
                             (v[:, :, 256:512], ret_w_in[l, :, 1024 + hd * 256:1024 + (hd + 1) * 256].rearrange("(k p) n -> p k n", p=128))],
                    [], ringT[s], "ring%d" % s)
                res["qk"] = (s, v)
            if "v" in which:
                s = ring_alloc()
                v = rv(s).rearrange("p (k n) -> p k n", k=8)
                dma("pool", [(v, ret_w_in[l, :, 2048 + hd * 512:2048 + (hd + 1) * 512].rearrange("(k p) n -> p k n", p=128))],
                    [], ringT[s], "ring%d" % s)
                res["v"] = (s, v)
            if "g" in which:
                s = ring_alloc()
                v = rv(s).rearrange("p (k n) -> p k n", k=8)
                dma("pool", [(v, ret_w_in[l, :, 4096 + hd * 512:4096 + (hd + 1) * 512].rearrange("(k p) n -> p k n", p=128))],
                    [], ringT[s], "ring%d" % s)
                res["g"] = (s, v)
            if "o" in which:
                s = ring_alloc()
                v = rv(s).rearrange("p (c n) -> p c n", c=4)
                dma("pool", [(v, ret_w_o[l, hd * 512:(hd + 1) * 512, :].rearrange("(c p) n -> p c n", p=128))],
                    [], ringT[s], "ring%d" % s)
                res["o"] = (s, v)
            return res

        onv = [hnb[i].bitcast(F32) for i in range(2)]
        qTs = [longs[0][:, i * 1024:(i + 1) * 1024].rearrange("p (a n) -> p a n", a=2) for i in range(2)]
        kTs = [longs[1][:, i * 1024:(i + 1) * 1024].rearrange("p (a n) -> p a n", a=2) for i in range(2)]
        qTT = [lqT[0], lqT[1]]
        kTT = [lkT[0], lkT[1]]
        Ws = {}
        st = {}

        def qkproj_pe(hd, g):
            sQK, wQK = Ws[hd]["qk"]
            (b0, nb) = TGS[g]
            N = nb * 128
            cols = slice(b0 * 128, b0 * 128 + N)
            hr = [hnTT[b0 + i] for i in range(nb)]
            dma("sp", [(Fb[0][:, 0:N], ret_cosT[:, cols])], [], [FT[0]], "f0")
            dma("sp", [(Fb[1][:, 0:N], ret_sinT[:, cols])], [], [FT[1]], "f1")
            coff = 0
            for k in range(8):
                mm(ps[:, 1, 0:N], wQK[:, k, coff:coff + 128], hnT[:, k, cols], k == 0, k == 7, ringT[sQK] + hr, [pT[1]])
            act(Fb[2][:, 0:N], ps[:, 1, 0:N], AF.Copy, [pT[1]], [FT[2]])
            for k in range(8):
                mm(ps[:, 2, 0:N], wQK[:, k, coff + 128:coff + 256], hnT[:, k, cols], k == 0, k == 7, ringT[sQK] + hr, [pT[2]])
            act(Fb[3][:, 0:N], ps[:, 2, 0:N], AF.Copy, [pT[2]], [FT[3]])

        def qkproj_pe2_mm(hd, g):
            sQK, wQK = Ws[hd]["qk"]
            (b0, nb) = TGS[g]
            N = nb * 128
            cols = slice(b0 * 128, b0 * 128 + N)
            hr = [hnTT[b0 + i] for i in range(nb)]
            coff = 256
            for k in range(8):
                mm(ps[:, 6, 0:N], wQK[:, k, coff:coff + 128], hnT[:, k, cols], k == 0, k == 7, ringT[sQK] + hr, [pT[6]])
            for k in range(8):
                mm(ps[:, 7, 0:N], wQK[:, k, coff + 128:coff + 256], hnT[:, k, cols], k == 0, k == 7, ringT[sQK] + hr, [pT[7]])

        def qkproj_pe2_copy(hd, g):
            (b0, nb) = TGS[g]
            N = nb * 128
            act(Fb[2][:, 0:N], ps[:, 6, 0:N], AF.Copy, [pT[6]], [FT[2]])
            act(Fb[3][:, 0:N], ps[:, 7, 0:N], AF.Copy, [pT[7]], [FT[3]])

        def qkproj_pe2(hd, g):
            qkproj_pe2_mm(hd, g)
            qkproj_pe2_copy(hd, g)

        def qkproj_ew(hd, g, which):
            (b0, nb) = TGS[g]
            N = nb * 128
            bi_ = (hd * len(TGS) + g) % 2
            dst, dT = (qTs[bi_], qTT[bi_]) if which == 0 else (kTs[bi_], kTT[bi_])
            tt("pool", Fb[4][:, 0:N], Fb[2][:, 0:N], Fb[0][:, 0:N], ALU.mult, [FT[2], FT[0]], [FT[4]])
            tt("pool", Fb[5][:, 0:N], Fb[3][:, 0:N], Fb[1][:, 0:N], ALU.mult, [FT[3], FT[1]], [FT[5]])
            tt("pool", dst[:, 0, 0:N], Fb[4][:, 0:N], Fb[5][:, 0:N], ALU.subtract, [FT[4], FT[5]], [dT])
            tt("dve", Fj[:, 0:N], Fb[2][:, 0:N], Fb[1][:, 0:N], ALU.mult, [FT[2], FT[1]], [junkT])
            tt("dve", Fb[2][:, 0:N], Fb[3][:, 0:N], Fb[0][:, 0:N], ALU.mult, [FT[3], FT[0]], [FT[2]])
            tt("dve", dst[:, 1, 0:N], Fj[:, 0:N], Fb[2][:, 0:N], ALU.add, [junkT, FT[2]], [dT])

        def qkproj(hd, g):
            qkproj_pe(hd, g)
            qkproj_ew(hd, g, 0)
            qkproj_pe2(hd, g)
            qkproj_ew(hd, g, 1)

        def P1a(key):
            hd, t, g, tb = key
            bi_ = (hd * len(TGS) + g) % 2
            bc = blk(tb)
            qT, kT = qTs[bi_], kTs[bi_]
            for dt in range(2):
                mm(ps[:, 0, 0:128], kT[:, dt, bc], qT[:, dt, bc], dt == 0, dt == 1, [qTT[bi_], kTT[bi_]], [pT[0]])
            si = Ball.get()
            tt("dve", Bb[si][:, 0:128], ps[:, 0, 0:128], MTp[:, hd, :], ALU.mult, [pT[0], constT], [BT[si]])
            st[key] = dict(si=si, qT=qT, bi=bi_, bc=bc)

        def P1b(key):
            hd, t, g, tb = key
            W = Ws[hd]
            sV, wV = W["v"]
            sG, wG = W["g"]
            d = st[key]
            bi_, bc = d["bi"], d["bc"]
            kT = kTs[bi_]
            for k in range(8):
                mm(ps[:, 1, :], hnT[:, k, blk(t)], wV[:, k, :], k == 0, k == 7, [hnTT[t]] + ringT[sV], [pT[1]])
            vi = Ball.get()
            act(Bb[vi][:], ps[:, 1, :], AF.Copy, [pT[1]], [BT[vi]])
            pk = psb[:, 0, 0:256].rearrange("p (a n) -> p a n", a=2)
            for dt in range(2):
                tr(pk[:, dt, :], kT[:, dt, bc], [kTT[bi_]], [pT[0]])
            ki = Ball.get()
            act(Bb[ki][:, 0:256], psb[:, 0, 0:256], AF.Identity, [pT[0], constT], [BT[ki]], scale=ccols[:, hd:hd + 1])
            for k in range(8):
                mm(ps[:, 2, :], hnT[:, k, blk(t)], wG[:, k, :], k == 0, k == 7, [hnTT[t]] + ringT[sG], [pT[2]])
            gi = Ball.get()
            act(Bb[gi][:], ps[:, 2, :], AF.Silu, [pT[2]], [BT[gi]])
            d.update(ki=ki, vi=vi, gi=gi)

        def P2(key):
            hd, t, g, tb = key
            d = st[key]
            ki, si, vi, qT, bi_, bc = d["ki"], d["si"], d["vi"], d["qT"], d["bi"], d["bc"]
            if t == 0:
                S.op("dve", lambda e: e.memset(Sst[:], 0.0), [], [SstT])
                S.op("pool", lambda e: e.memset(Sbf[:], 0.0), [], [SbfT])
            mm(ps[:, 3, :], Bb[si][:, 0:128], Bb[vi][:], True, False, [BT[si], BT[vi]], [pT[3]])
            for dt in range(2):
                mm(ps[:, 3, :], qT[:, dt, bc], Sbf[:, dt, :], False, dt == 1, [qTT[bi_], SbfT], [pT[3]])
            for dt in range(2):
                mm(ps[:, 4 + dt, :], Bb[ki][:, dt * 128:(dt + 1) * 128], Bb[vi][:], True, True,
                   [BT[ki], BT[vi]], [pT[4 + dt]])
            s1 = ST.get()
            sa = stt_[:, s1, :]
            S.op("dve", lambda e, sa=sa: e.bn_stats(out=sa[:, 0:6], in_=ps[:, 3, :]), [pT[3]], [stT[s1]])
            s2 = ST.get()
            sq = stt_[:, s2, :]
            S.op("dve", lambda e, sa=sa, sq=sq: e.bn_aggr(out=sq[:, 0:2], in_=sa[:, 0:6]), [stT[s1]], [stT[s2]])
            tt("dve", sq[:, 2:3], sq[:, 1:2], ccols[:, 4 + hd:5 + hd], ALU.add, [stT[s2], constT], [stT[s2]])
            rsqrt_col(sq[:, 3:4], sq[:, 2:3], [stT[s2]], [stT[s2]])
            stt("dve", Sst[:], Sst[:], gam[hd], ps[:, 4:6, :], ALU.mult, ALU.add,
                [SstT, pT[4], pT[5]], [SstT])
            act(Sbf[:], Sst[:], AF.Copy, [SstT], [SbfT])
            d.update(s2=s2, sq=sq)

        def GN(key):
            d = st[key]
            s2, sq, gi = d["s2"], d["sq"], d["gi"]
            stt("dve", sq[:, 4:5], sq[:, 0:1], -1.0, sq[:, 3:4], ALU.mult, ALU.mult, [stT[s2]], [stT[s2]])
            oi = HNB.get()
            act(onv[oi][:, :], ps[:, 3, :], AF.Identity, [pT[3], stT[s2]], [hnbT[oi]], scale=sq[:, 3:4], bias=sq[:, 4:5])
            yi = Ball.get()
            tt("dve", Bb[yi][:], onv[oi][:, :], Bb[gi][:], ALU.mult, [hnbT[oi], BT[gi]], [BT[yi]])
            d.update(yi=yi)

        def P3(key):
            hd = key[0]
            d = st[key]
            yi = d["yi"]
            py = psb[:, 2, 0:512].rearrange("p (c n) -> p c n", c=4)
            for c in range(4):
                tr(py[:, c, :], Bb[yi][:, c * 128:(c + 1) * 128], [BT[yi]], [pT[2]])
            yti = Ball.get()
            gc = G_GN + l * 16 + hd * 4
            tt("dve", Bb[yti][:].rearrange("p (c n) -> p c n", c=4), py,
               gfm[:, gc:gc + 4].unsqueeze(2).broadcast_to([128, 4, 128]), ALU.mult, [pT[2], constT], [BT[yti]])
            d.update(yti=yti)

        def P4(key):
            hd, t, g, tb = key
            sO, wO = Ws[hd]["o"]
            yti = st[key]["yti"]
            for half in range(2):
                for c in range(4):
                    mm(ps[:, 6 + half, :], Bb[yti][:, c * 128:(c + 1) * 128], wO[:, c, half * 512:(half + 1) * 512],
                       c == 0, c == 3, [BT[yti]] + ringT[sO], [pT[6 + half]])
            tt("dve", h[:, t, :].rearrange("p (a n) -> p a n", a=2),
               h[:, t, :].rearrange("p (a n) -> p a n", a=2), ps[:, 6:8, :], ALU.add,
               [hT[t], pT[6], pT[7]], [hT[t]])
            del st[key]

        items = []
        for hd in range(4):
            for g, (b0, nb) in enumerate(TGS):
                for tb in range(nb):
                    items.append((hd, b0 + tb, g, tb))
        Ws[0] = load(0, ("qk", "v", "g", "o"))
        Ws[1] = load(1, ("qk", "v", "g"))
        if norm is not None:
            norm()
        qkproj(0, 0)
        n = len(items)
        P1a(items[0])
        pending_k = None
        for i, key in enumerate(items):
            hd, t, g, tb = key
            nbk = TGS[g][1]
            nxt_qk = None
            if tb == max(nbk - 3, 0):
                if g + 1 < len(TGS):
                    nxt_qk = (hd, g + 1)
                elif hd + 1 < 4:
                    nxt_qk = (hd + 1, 0)
            do_k = pending_k
            pending_k = None
            if nxt_qk is not None:
                qkproj_pe(*nxt_qk)
            elif do_k is not None:
                qkproj_pe2_mm(*do_k)
            P1b(key)
            P2(key)
            if i >= 1:
                P3(items[i - 1])
            if nxt_qk is None and do_k is not None:
                qkproj_pe2_copy(*do_k)
            if i >= 2:
                k2 = items[i - 2]
                P4(k2)
                if k2[1] == NB - 1 and k2[0] + 1 < 4:
                    Ws[k2[0] + 1].update(load(k2[0] + 1, ("o",)))
                    if k2[0] + 2 < 4:
                        Ws[k2[0] + 2] = load(k2[0] + 2, ("qk", "v", "g"))
            early = nxt_qk is not None and nbk == 1
            if early:
                qkproj_ew(nxt_qk[0], nxt_qk[1], 0)
                qkproj_pe2(*nxt_qk)
                qkproj_ew(nxt_qk[0], nxt_qk[1], 1)
            if i + 1 < n:
                P1a(items[i + 1])
            GN(key)
            if nxt_qk is not None and not early:
                qkproj_ew(nxt_qk[0], nxt_qk[1], 0)
                pending_k = nxt_qk
            elif do_k is not None:
                qkproj_ew(do_k[0], do_k[1], 1)
        P3(items[n - 1])
        P4(items[n - 2])
        P4(items[n - 1])

    def kv_phase():
        sA = ring_alloc()
        sB = ring_alloc()
        wA = rv(sA).rearrange("p (k n) -> p k n", k=8)
        wB = rv(sB)[:, 0:512].rearrange("p (k n) -> p k n", k=8)
        dma("pool", [(wA, w_kv_a[:, 0:512].rearrange("(k p) n -> p k n", p=128))], [], ringT[sA], "ring%d" % sA)
        dma("pool", [(wB, w_kv_a[:, 512:576].rearrange("(k p) n -> p k n", p=128))], [], ringT[sB], "ring%d" % sB)
        ktab = longs[0].bitcast(F32)
        dma("sp", [(ktab[:, :], mla_ktab)], [], [lqT[0], lqT[1]], "long0")
        ktv = ktab[:, :].rearrange("p (t c) -> p t c", c=64)
        norm_phase(G_KV)

        def kmm(t):
            dd = PDD.get()
            for k in range(8):
                mm(ps[:, dd[0], :], hnT[:, k, blk(t)], wA[:, k, :], k == 0, k == 7, [hnTT[t]] + ringT[sA], [pT[dd[0]]])
            for k in range(8):
                mm(ps[:, dd[1], 0:64], hnT[:, k, blk(t)], wB[:, k, :], k == 0, k == 7, [hnTT[t]] + ringT[sB], [pT[dd[1]]])
            return dd

        def krest(t, dd):
            s1 = ST.get()
            st = stt_[:, s1, :]
            act(junk[:, 0:512], ps[:, dd[0], :], AF.Square, [pT[dd[0]]], [stT[s1], junkT], accum=st[:, 0:1])
            ts("dve", st[:, 1:2], st[:, 0:1], 1.0 / 512, RMS_EPS, ALU.mult, ALU.add, [stT[s1]], [stT[s1]])
            rsqrt_col(st[:, 2:3], st[:, 1:2], [stT[s1]], [stT[s1]])
            ci = Ball.get()
            act(Bb[ci][:], ps[:, dd[0], :], AF.Identity, [pT[dd[0]], stT[s1]], [BT[ci]], scale=st[:, 2:3])
            bk = PA.get()
            pc = psb[:, bk, 0:512].rearrange("p (c n) -> p c n", c=4)
            for c in range(4):
                tr(pc[:, c, :], Bb[ci][:, c * 128:(c + 1) * 128], [BT[ci]], [pT[bk]])
            tt("dve", ckvT[:, :, blk(t)], pc, gfm[:, G_KVA:G_KVA + 4].unsqueeze(2).broadcast_to([128, 4, 128]),
               ALU.mult, [pT[bk], constT], [ckvTT[t]])
            fi = Frot.get()
            tmp = Fb[fi]
            x1 = ps[:, dd[1], 0:32]
            x2 = ps[:, dd[1], 32:64]
            cc_ = ktv[:, t, 0:32]
            ss_ = ktv[:, t, 32:64]
            rd = [pT[dd[1]], lqT[0], lqT[1]]
            tt("dve", tmp[:, 0:32], x1, cc_, ALU.mult, rd, [FT[fi]])
            tt("dve", tmp[:, 32:64], x2, ss_, ALU.mult, rd, [FT[fi]])
            tt("dve", tmp[:, 64:96], x1, ss_, ALU.mult, rd, [FT[fi]])
            tt("dve", tmp[:, 96:128], x2, cc_, ALU.mult, rd, [FT[fi]])
            ki = Ball.get()
            tt("pool", Bb[ki][:, 0:32], tmp[:, 0:32], tmp[:, 32:64], ALU.subtract, [FT[fi]], [BT[ki]])
            tt("pool", Bb[ki][:, 32:64], tmp[:, 64:96], tmp[:, 96:128], ALU.add, [FT[fi]], [BT[ki]])
            tt("pool", Bb[ki][:, 64:96], tmp[:, 0:32], tmp[:, 32:64], ALU.subtract, [FT[fi]], [BT[ki]])
            tt("pool", Bb[ki][:, 96:128], tmp[:, 64:96], tmp[:, 96:128], ALU.add, [FT[fi]], [BT[ki]])
            bk2 = PA.get()
            tr(psb[:, bk2, 0:128], Bb[ki][:, 0:128], [BT[ki]], [pT[bk2]])
            act(kropeT[:, blk(t)], psb[:, bk2, 0:128], AF.Copy, [pT[bk2]], [krTT[t]])

        dds = {0: kmm(0)}
        for t in range(NB):
            if t + 1 < NB:
                dds[t + 1] = kmm(t + 1)
            krest(t, dds.pop(t))

    def mla(j):
        l = 2 + j
        sA = ring_alloc()
        sB = ring_alloc()
        wA = rv(sA).rearrange("p (k n) -> p k n", k=8)
        wB = rv(sB)[:, 0:2048].rearrange("p (k n) -> p k n", k=8)
        dma("pool", [(wA, w_q_a[j, :, 0:512].rearrange("(k p) n -> p k n", p=128))], [], ringT[sA], "ring%d" % sA)
        dma("pool", [(wB, w_q_a[j, :, 512:768].rearrange("(k p) n -> p k n", p=128))], [], ringT[sB], "ring%d" % sB)
        norm_phase(G_MIX + l * 8)

        def qmm(t):
            dd = PDD.get()
            for k in range(8):
                mm(ps[:, dd[0], :], hnT[:, k, blk(t)], wA[:, k, :], k == 0, k == 7, [hnTT[t]] + ringT[sA], [pT[dd[0]]])
            for k in range(8):
                mm(ps[:, dd[1], 0:256], hnT[:, k, blk(t)], wB[:, k, :], k == 0, k == 7, [hnTT[t]] + ringT[sB], [pT[dd[1]]])
            return dd

        def qrest(t, dd):
            s1 = ST.get()
            st = stt_[:, s1, :]
            act(junk[:, 0:512], ps[:, dd[0], :], AF.Square, [pT[dd[0]]], [stT[s1], junkT], accum=st[:, 0:1])
            act(junk[:, 512:768], ps[:, dd[1], 0:256], AF.Square, [pT[dd[1]]], [stT[s1], junkT], accum=st[:, 1:2])
            tt("dve", st[:, 2:3], st[:, 0:1], st[:, 1:2], ALU.add, [stT[s1]], [stT[s1]])
            ts("dve", st[:, 3:4], st[:, 2:3], 1.0 / 768, RMS_EPS, ALU.mult, ALU.add, [stT[s1]], [stT[s1]])
            rsqrt_col(st[:, 4:5], st[:, 3:4], [stT[s1]], [stT[s1]])
            hbi = HNB.get()
            hb = hnb[hbi]
            act(hb[:, 0:512], ps[:, dd[0], :], AF.Identity, [pT[dd[0]], stT[s1]], [hnbT[hbi]], scale=st[:, 4:5])
            ts("dve", hb[:, 512:768], ps[:, dd[1], 0:256], st[:, 4:5], None, ALU.mult, None, [pT[dd[1]], stT[s1]], [hnbT[hbi]])
            bk = PA.get()
            pc = psb[:, bk, 0:768].rearrange("p (c n) -> p c n", c=6)
            for c in range(6):
                tr(pc[:, c, :], hb[:, c * 128:(c + 1) * 128], [hnbT[hbi]], [pT[bk]])
            gc = G_QA + j * 6
            tt("dve", hnT[:, 0:6, blk(t)], pc, gfm[:, gc:gc + 6].unsqueeze(2).broadcast_to([128, 6, 128]),
               ALU.mult, [pT[bk], constT], [hnTT[t]])

        dds = {0: qmm(0)}
        for t in range(NB):
            if t + 1 < NB:
                dds[t + 1] = qmm(t + 1)
            qrest(t, dds.pop(t))

        def load(hh):
            s = ring_alloc()
            wqb = rv(s)[:, 0:1536].rearrange("p (k n) -> p k n", k=6)
            wkvb = rv(s)[:, 1536:2560].rearrange("p (k n) -> p k n", k=4)
            wo = rv(s)[:, 2560:3584]
            c0 = hh * 192
            dma("pool", [(wqb[:, :, 0:192], w_q_b[j, :, c0:c0 + 192].rearrange("(k p) n -> p k n", p=128)),
                         (wqb[:, :, 192:224], w_q_b[j, :, c0 + 160:c0 + 192].rearrange("(k p) n -> p k n", p=128)),
                         (wqb[:, :, 224:256], w_q_b[j, :, c0 + 128:c0 + 160].rearrange("(k p) n -> p k n", p=128)),
                         (wkvb, w_kv_b[:, hh * 256:(hh + 1) * 256].rearrange("(k p) n -> p k n", p=128)),
                         (wo, mla_w_o[j, hh * 128:(hh + 1) * 128, :])],
                [], ringT[s], "ring%d" % s)
            return (s, wqb, wkvb, wo)

        LOOK = 2

        def compute(hh, W):
            s, wqb, wkvb, wo = W
            knT = longs[0]
            vh = longs[1][:, :].rearrange("p (t n) -> p t n", n=128)
            knTT = [lqT[0], lqT[1]]
            vhT = [lkT[0], lkT[1]]
            rT = ringT[s]

            def kvproj():
              for (b0, nb) in TGS:
                N = nb * 128
                cols = slice(b0 * 128, b0 * 128 + N)
                cr = [ckvTT[b0 + i] for i in range(nb)]
                bk = PA3.get()
                for kt in range(4):
                    mm(ps[:, bk, 0:N], wkvb[:, kt, 0:128], ckvT[:, kt, cols], kt == 0, kt == 3, rT + cr, [pT[bk]])
                act(knT[:, cols], ps[:, bk, 0:N], AF.Copy, [pT[bk]], knTT)
                bv = PA3.get()
                pvv = ps[:, bv, :].rearrange("p (t n) -> p t n", n=128)
                for tb in range(nb):
                    for kt in range(4):
                        mm(pvv[:, tb, :], ckvT[:, kt, blk(b0 + tb)], wkvb[:, kt, 128:256], kt == 0, kt == 3,
                           rT + [ckvTT[b0 + tb]], [pT[bv]])
                act(vh[:, b0:b0 + nb, :], pvv[:, 0:nb, :], AF.Copy, [pT[bv]], vhT)
            Fr2 = RPool([2, 3])
            state = {}

            def stageA(g):
                (b0, nb) = TGS[g]
                N = nb * 128
                cols = slice(b0 * 128, b0 * 128 + N)
                hr = [hnTT[b0 + i] for i in range(nb)]
                dma("sp", [(Fb[0][:, 0:N], mla_CS[:, cols])], [], [FT[0]], "f0")
                qni = g % 2
                qri = 2 + g % 2
                bq = PA3.get()
                for kt in range(6):
                    mm(ps[:, bq, 0:N], wqb[:, kt, 0:128], hnT[:, kt, cols], kt == 0, kt == 5, rT + hr, [pT[bq]])
                act(Bb[qni][:, 0:N], ps[:, bq, 0:N], AF.Identity, [pT[bq]], [BT[qni]], scale=float(192.0 ** -0.5))
                ba = PA3.get()
                for kt in range(6):
                    mm(ps[:, ba, 0:N], wqb[:, kt, 128:256], hnT[:, kt, cols], kt == 0, kt == 5, rT + hr, [pT[ba]])
                tt("dve", Bb[qri][:, 0:N], ps[:, ba, 0:N], Fb[0][:, 0:N], ALU.mult, [pT[ba], FT[0]], [BT[qri]])

            def stageB(g, pend=None):
                pend = pend if pend is not None else []
                (b0, nb) = TGS[g]
                N = nb * 128
                qni = g % 2
                qri = 2 + g % 2
                last = b0 + nb - 1
                chunks = []
                for jb in range(0, last + 1):
                    i0 = max(jb, b0)
                    chunks.append((jb, (last - i0 + 1) * 128, (i0 - b0) * 128))

                def STc(c):
                    jb, ncol, q0 = c
                    bs = PS3.get()
                    mm(ps[:, bs, 0:ncol], knT[:, blk(jb)], Bb[qni][:, q0:q0 + ncol], True, False,
                       knTT + [BT[qni]], [pT[bs]])
                    mm(ps[:, bs, 0:ncol], kropeT[:, blk(jb)], Bb[qri][:, q0:q0 + ncol], False, True,
                       [krTT[jb], BT[qri]], [pT[bs]])
                    pi = PTB.get()
                    pb, pbT = pi
                    act(pb[:, 0:ncol], ps[:, bs, 0:ncol], AF.Exp, [pT[bs]], [pbT])
                    if jb >= b0:
                        act(pb[64:128, 0:64], ones[64:128, 0:64], AF.Copy, [constT], [pbT], scale=0.0)
                    return pi

                def PVc(c, pi):
                    jb, ncol, q0 = c
                    pb, pbT = pi
                    mm(ps[:, 4, q0:q0 + ncol], vh[:, jb, :], pb[:, 0:ncol], jb == 0, jb == last,
                       vhT + [pbT], [pT[4]])
                    vl = valid0 if jb == 0 else ones
                    mm(ps[:, 5, q0:q0 + ncol], vl[:], pb[:, 0:ncol], jb == 0, jb == last,
                       [constT, pbT], [pT[5]])

                n = len(chunks)
                pis = {}
                for k in range(min(LOOK, n)):
                    pis[k] = STc(chunks[k])
                for k in range(n):
                    if k + LOOK < n:
                        pis[k + LOOK] = STc(chunks[k + LOOK])
                    PVc(chunks[k], pis.pop(k))
                    if pend:
                        pend.pop(0)()
                while pend:
                    pend.pop(0)()

            def stageC1(g):
                (b0, nb) = TGS[g]
                N = nb * 128
                fr = 4 + g % 2
                S.op("dve", lambda e, fr=fr, N=N: e.reciprocal(out=Fb[fr][:, 0:N], in_=ps[:, 5, 0:N]), [pT[5]], [FT[fr]])
                oi = 6 + g % 2
                tt("dve", Bb[oi][:, 0:N], ps[:, 4, 0:N], Fb[fr][:, 0:N], ALU.mult, [pT[4], FT[fr]], [BT[oi]])

            def stageC2(g):
                (b0, nb) = TGS[g]
                oi = 6 + g % 2
                items = []
                for tb in range(nb):
                    for half in range(2):
                        def item(tb=tb, half=half):
                            t = b0 + tb
                            bw = PA3.get()
                            mm(ps[:, bw, :], Bb[oi][:, blk(tb)], wo[:, half * 512:(half + 1) * 512], True, True,
                               [BT[oi]] + rT, [pT[bw]])
                            hs = h[:, t, half * 512:(half + 1) * 512]
                            tt("dve", hs, hs, ps[:, bw, :], ALU.add, [hT[t], pT[bw]], [hT[t]])
                        items.append(item)
                return items

            return kvproj, stageA, stageB, stageC1, stageC2

        G = len(TGS)
        alias_in(hnbH, hnbT)
        Wd = {0: load(0), 1: load(1)}
        cur = compute(0, Wd[0])
        cur[0]()
        cur[1](0)
        cur[1](1)
        pend = []
        for hh in range(8):
            kvp, sA_, sB_, sC1_, sC2_ = cur
            if hh + 2 < 8:
                Wd[hh + 2] = load(hh + 2)
            nxt = None
            for g in range(G):
                sB_(g, pend)
                sC1_(g)
                if g + 2 < G:
                    sA_(g + 2)
                if g == G - 1 and hh + 1 < 8:
                    nxt = compute(hh + 1, Wd[hh + 1])
                    nxt[0]()
                    nxt[1](0)
                    nxt[1](1)
                pend = sC2_(g)
            cur = nxt
        while pend:
            pend.pop(0)()
        alias_out(hnbH, hnbT)

    def final_phase(b):
        dma("sp", [(Fb[0][:], final_g[:, 0:512].broadcast_to([128, 512]))], [], [FT[0]], "f0")
        dma("sp", [(Fb[1][:], final_g[:, 512:1024].broadcast_to([128, 512]))], [], [FT[1]], "f1")
        for t in range(1, NB):
            hbi = HNB.get()
            act(hnb[hbi][:], h[:, t, :], AF.Square, [hT[t]], [hnbT[hbi], ssqT[t]], accum=ssq[:, t:t + 1])
            ts("dve", ssq[:, t:t + 1], ssq[:, t:t + 1], 1.0 / D, RMS_EPS, ALU.mult, ALU.add, [ssqT[t]], [ssqT[t]])
            rsqrt_col(ssq[:, NB + t:NB + t + 1], ssq[:, t:t + 1], [ssqT[t]], [ssqT[NB + t]])
            for half in range(2):
                hs = h[:, t, half * 512:(half + 1) * 512]
                stt("dve", hs, hs, ssq[:, NB + t:NB + t + 1], Fb[half][:], ALU.mult, ALU.mult,
                    [hT[t], ssqT[NB + t], FT[half]], [hT[t]])
        for c in range(4):
            dma("sp", [(out[b, c * 512:(c + 1) * 512, :].rearrange("(t p) d -> p t d", p=128), h[:, 1 + 4 * c:5 + 4 * c, :])],
                [hT[1 + 4 * c + i] for i in range(4)], [], "hout%d" % c)

    for b in range(nseq):
        seq_begin(b)
        nring[0] = 7
        step = 0
        for l in range(2):
            if step < stop_after:
                retention(l, norm=lambda l=l: norm_phase(G_MIX + l * 8))
            step += 1
            if step < stop_after:
                ffn(l, norm=lambda l=l: norm_phase(G_FFN + l * 8))
            step += 1
        nring[0] = 5
        if step < stop_after:
            kv_phase()
        step += 1
        for j in range(2):
            if step < stop_after:
                mla(j)
            step += 1
            if step < stop_after:
                ffn(2 + j, norm=lambda j=j: norm_phase(G_FFN + (2 + j) * 8))
            step += 1
        final_phase(b)

    S.emit(final_waits=["hout%d" % c for c in range(4)])
    es.close()
    return nc, S


_CACHE = {}


def kernel(x, meta, norm_mix_g, norm_ffn_g, ret_w_in, ret_gn_g, ret_w_o,
           mla_norm_kv_g, mla_w_kv_a, mla_kv_a_norm_g, mla_w_kv_b,
           mla_w_q_a, mla_q_a_norm_g, mla_w_q_b, mla_w_o,
           ffn_w1, ffn_w3, ffn_w2, final_g, _nseq=None, _stop_after=9, _ncores=None):
    f = lambda a: np.ascontiguousarray(np.asarray(a, dtype=np.float32))
    x = f(x)
    B = x.shape[0]
    ncores = _ncores or NCORES
    nseq = _nseq or (B // ncores)
    consts, gam128 = _consts()
    key = (nseq, _stop_after)
    if key not in _CACHE:
        _CACHE[key] = build_nc(nseq, _stop_after, gam128)[0]
    nc = _CACHE[key]
    gfm = np.zeros((128, G_TOT), np.float32)
    nm, nf = f(norm_mix_g), f(norm_ffn_g)
    for l in range(4):
        gfm[:, G_MIX + l * 8:G_MIX + (l + 1) * 8] = _fm(nm[l], 8)
        gfm[:, G_FFN + l * 8:G_FFN + (l + 1) * 8] = _fm(nf[l], 8)
    gfm[:, G_KV:G_KV + 8] = _fm(mla_norm_kv_g, 8)
    gn = f(ret_gn_g)
    for l in range(2):
        gfm[:, G_GN + l * 16:G_GN + (l + 1) * 16] = _fm(gn[l], 16)
    qa = f(mla_q_a_norm_g)
    for j in range(2):
        gfm[:, G_QA + j * 6:G_QA + (j + 1) * 6] = _fm(qa[j], 6)
    gfm[:, G_KVA:G_KVA + 4] = _fm(mla_kv_a_norm_g, 4)
    shared = {
        "meta": f(meta), "ret_w_in": f(ret_w_in), "ret_w_o": f(ret_w_o),
        "mla_w_kv_a": f(mla_w_kv_a), "mla_w_kv_b": f(mla_w_kv_b), "mla_w_q_a": f(mla_w_q_a),
        "mla_w_q_b": f(mla_w_q_b), "mla_w_o": f(mla_w_o), "ffn_w1": f(ffn_w1), "ffn_w3": f(ffn_w3),
        "ffn_w2": f(ffn_w2), "final_g": f(final_g).reshape(1, D), "gfm": gfm,
        "ret_cosT": consts["ret_cosT"], "ret_sinT": consts["ret_sinT"], "mla_CS": consts["mla_CS"], "mla_ktab": consts["mla_ktab"].reshape(128, NB * 64),
        "ident": consts["ident"], "ones": consts["ones"], "valid0": consts["valid0"],
        "MTp": consts["MTp"].reshape(128, 512), "ccols": consts["ccols"],
    }
    in_maps = []
    for c in range(ncores):
        m = dict(shared)
        m["x"] = np.ascontiguousarray(x[c * nseq:(c + 1) * nseq])
        in_maps.append(m)
    res = run_bass_kernel_spmd(nc, in_maps, core_ids=list(range(ncores)))
    return np.concatenate([np.asarray(r["out"], dtype=np.float32) for r in res.results], axis=0)
```

```python
import numpy as np
import concourse.bass as bass
import concourse.mybir as mybir
from concourse.bass_utils import run_bass_kernel_spmd
from contextlib import ExitStack

F32 = mybir.dt.float32
BF16 = mybir.dt.bfloat16
AF = mybir.ActivationFunctionType
ALU = mybir.AluOpType

D = 1024
SEQ = 2048
PAD = 112
L = 2176
NB = 17
NCORES = 8
BATCH = 32
HID = 2816
RMS_EPS = 1e-6
GN_EPS = 1e-5
TGS = [(0, 4), (4, 4), (8, 4), (12, 4), (16, 1)]
NRING = 5

G_MIX = 0
G_FFN = 32
G_KV = 64
G_GN = 72
G_QA = 104
G_KVA = 116
G_TOT = 120

ENGS = ("pe", "act", "dve", "pool", "sp")


class Tile:
    __slots__ = ("name", "w", "rd")

    def __init__(self, name):
        self.name = name
        self.w = None
        self.rd = []


class Op:
    __slots__ = ("eng", "fn", "deps", "sig", "cnt", "dma", "dsem", "dcnt", "idx", "waits")

    def __init__(self, eng, fn):
        self.eng = eng
        self.fn = fn
        self.deps = []
        self.sig = False
        self.cnt = 0
        self.dma = 0
        self.dsem = None
        self.dcnt = 0
        self.waits = None


class Sched:
    def __init__(self, nc):
        self.nc = nc
        self.ops = []
        self.dma_sems = {}

    def op(self, eng, fn, reads=(), writes=(), dma=0, dsem=None):
        o = Op(eng, fn)
        o.dma = dma
        deps = {}
        for t in reads:
            if t.w is not None:
                deps[id(t.w)] = t.w
        for t in writes:
            if t.w is not None:
                deps[id(t.w)] = t.w
            for r in t.rd:
                deps[id(r)] = r
        for d in deps.values():
            if d is o:
                continue
            if d.dma == 0 and d.eng == eng and eng == "pe" and dma == 0:
                continue
            o.deps.append(d)
        for t in reads:
            t.rd.append(o)
        for t in writes:
            t.w = o
            t.rd = []
        if dma:
            self.dma_sems[dsem] = self.dma_sems.get(dsem, 0) + dma
            o.dsem = dsem
            o.dcnt = self.dma_sems[dsem]
        o.idx = len(self.ops)
        self.ops.append(o)
        return o

    def finalize(self):
        for o in self.ops:
            for d in o.deps:
                if d.dma == 0:
                    d.sig = True
        cnt = {e: 0 for e in ENGS}
        for o in self.ops:
            if o.dma == 0 and o.sig:
                cnt[o.eng] += 1
                o.cnt = cnt[o.eng]
        ei = {e: i for i, e in enumerate(ENGS)}
        ne = len(ENGS)
        known = {e: [0] * ne for e in ENGS}
        known_d = {e: {} for e in ENGS}
        clocks = [None] * len(self.ops)
        for o in self.ops:
            k = known[o.eng]
            kd = known_d[o.eng]
            best = {}
            for d in o.deps:
                if d.dma:
                    need = d.dcnt * 16
                    if kd.get(d.dsem, 0) < need:
                        kd[d.dsem] = need
                        best[("d", d.dsem)] = need
                    dc = clocks[d.idx]
                    for i in range(ne):
                        if dc[i] > k[i]:
                            k[i] = dc[i]
                else:
                    j = ei[d.eng]
                    if k[j] < d.cnt:
                        key = ("e", d.eng)
                        if best.get(key, 0) < d.cnt:
                            best[key] = d.cnt
                        dc = clocks[d.idx]
                        for i in range(ne):
                            if dc[i] > k[i]:
                                k[i] = dc[i]
                        if k[j] < d.cnt:
                            k[j] = d.cnt
            o.waits = [(a, b, v) for (a, b), v in best.items()]
            c = list(k)
            if o.dma == 0 and o.sig:
                c[ei[o.eng]] = max(c[ei[o.eng]], o.cnt)
            clocks[o.idx] = c
        return cnt

    def emit(self, final_waits=()):
        nc = self.nc
        self.finalize()
        with ExitStack() as es:
            esem = {e: es.enter_context(nc.semaphore("s_" + e)) for e in ENGS}
            dsem = {k: es.enter_context(nc.semaphore("d_%s" % (k,))) for k in self.dma_sems}
            block = es.enter_context(nc.Block())
            per = {e: [o for o in self.ops if o.eng == e] for e in ENGS}

            def run(eng_name, eng):
                for o in per[eng_name]:
                    for (kind, key, val) in o.waits:
                        if kind == "e":
                            eng.wait_ge(esem[key], val)
                        else:
                            eng.wait_ge(dsem[key], val)
                    if o.dma:
                        o.fn(eng, dsem[o.dsem])
                    else:
                        ins = o.fn(eng)
                        if o.sig:
                            ins.then_inc(esem[o.eng], 1)
                if eng_name == "sp":
                    for k in final_waits:
                        eng.wait_ge(dsem[k], self.dma_sems[k] * 16)

            @block.tensor
            def _(e):
                run("pe", e)

            @block.scalar
            def _(e):
                run("act", e)

            @block.vector
            def _(e):
                run("dve", e)

            @block.gpsimd
            def _(e):
                run("pool", e)

            @block.sync
            def _(e):
                run("sp", e)


class RPool:
    def __init__(self, items):
        self.items = items
        self.i = 0

    def get(self):
        it = self.items[self.i % len(self.items)]
        self.i += 1
        return it


def _consts():
    f32 = np.float32
    slot = np.arange(L)
    pos = (slot - PAD).astype(f32)

    def tables(dim):
        inv = (1.0 / (f32(10000.0) ** (np.arange(0, dim, 2, dtype=f32) / f32(dim)))).astype(f32)
        ang = (pos[:, None] * inv[None, :]).astype(f32)
        return np.cos(ang).astype(f32), np.sin(ang).astype(f32)

    cr, sr = tables(256)
    cm, sm = tables(64)
    c = {}
    c["ret_cosT"] = np.ascontiguousarray(cr.T)
    c["ret_sinT"] = np.ascontiguousarray(sr.T)
    sc = f32(192.0 ** -0.5)
    c["mla_CS"] = np.ascontiguousarray(np.concatenate([cm.T, cm.T, -sm.T, sm.T], 0) * sc).astype(f32)
    kt = np.concatenate([cm, sm], 1).reshape(NB, 128, 64).transpose(1, 0, 2)
    c["mla_ktab"] = np.ascontiguousarray(kt).astype(f32)
    c["ident"] = np.eye(128, dtype=f32)
    c["ones"] = np.ones((128, 128), f32)
    v0 = np.ones((128, 128), f32)
    v0[:PAD, :] = 1e-30
    c["valid0"] = v0
    H = 4
    log_g = np.log1p(-np.exp2(-5.0 - np.arange(H, dtype=np.float64)))
    i = np.arange(128, dtype=np.float64)
    MTp = np.zeros((128, H, 128), np.float64)
    cc = np.zeros((128, 16), np.float64)
    gam128 = []
    for h in range(H):
        lg = log_g[h]
        qdec = np.exp(lg * (i + 1.0))
        ii, jj = np.meshgrid(i, i, indexing="ij")
        same = (ii // 64) == (jj // 64)
        M = np.where(same, np.exp(lg * np.abs(ii - jj)),
                     np.where(ii > jj, np.exp(lg * (ii - jj)), 0.0))
        Mp = M / qdec[:, None] / 16.0
        MTp[:, h, :] = Mp.T
        cc[:, h] = np.exp(lg * (127.0 - i)) / 16.0
        cc[:, 4 + h] = GN_EPS / (qdec ** 2)
        gam128.append(float(np.exp(lg * 128.0)))
    cc[:, 8] = -0.5
    c["MTp"] = MTp.astype(f32)
    c["ccols"] = cc.astype(f32)
    return c, gam128


def _fm(g, k):
    return np.ascontiguousarray(np.asarray(g, np.float32).reshape(k, 128).T)


def build_nc(nseq=4, stop_after=9, gam128=None):
    nc = bass.Bass("TRN2", target_bir_lowering=False)

    def din(name, shape):
        return nc.dram_tensor(name, list(shape), F32, kind="ExternalInput").ap()

    x = din("x", [nseq, SEQ, D])
    meta = din("meta", [16, D])
    ret_w_in = din("ret_w_in", [2, D, 6144])
    ret_w_o = din("ret_w_o", [2, 2048, D])
    w_kv_a = din("mla_w_kv_a", [D, 576])
    w_kv_b = din("mla_w_kv_b", [512, 2048])
    w_q_a = din("mla_w_q_a", [2, D, 768])
    w_q_b = din("mla_w_q_b", [2, 768, 1536])
    mla_w_o = din("mla_w_o", [2, D, D])
    ffn_w1 = din("ffn_w1", [4, D, HID])
    ffn_w3 = din("ffn_w3", [4, D, HID])
    ffn_w2 = din("ffn_w2", [4, HID, D])
    final_g = din("final_g", [1, D])
    gfm_d = din("gfm", [128, G_TOT])
    ret_cosT = din("ret_cosT", [128, L])
    ret_sinT = din("ret_sinT", [128, L])
    mla_CS = din("mla_CS", [128, L])
    mla_ktab = din("mla_ktab", [128, NB * 64])
    ident_d = din("ident", [128, 128])
    ones_d = din("ones", [128, 128])
    valid0_d = din("valid0", [128, 128])
    MTp_d = din("MTp", [128, 4 * 128])
    ccols_d = din("ccols", [128, 16])
    out = nc.dram_tensor("out", [nseq, SEQ, D], F32, kind="ExternalOutput").ap()

    es = ExitStack()

    def sb(name, shape, dt):
        return es.enter_context(nc.sbuf_tensor(name, shape, dt))

    S = Sched(nc)

    h = sb("h_sb", [128, NB, D], F32)
    hT = [Tile("h%d" % t) for t in range(NB)]
    hnT = sb("hnT", [128, 8, L], BF16)
    hnTT = [Tile("hnT%d" % t) for t in range(NB)]
    ring = sb("ring", [128, NRING, 4096], BF16)
    ckvT = sb("ckvT", [128, 4, L], BF16)
    ckvTT = [Tile("ckv%d" % t) for t in range(NB)]
    ckv_flat = ckvT[:, :, :].rearrange("p k n -> p (k n)")
    ringT = [[Tile("ring%d" % s)] for s in range(NRING)] + [ckvTT, ckvTT]
    nring = [7]

    def rv(s):
        if s < NRING:
            return ring[:, s, :]
        return ckv_flat[:, (s - NRING) * 4096:(s - NRING + 1) * 4096]
    kropeT = sb("kropeT", [128, L], BF16)
    krTT = [Tile("kr%d" % t) for t in range(NB)]
    longs = [sb("long%d" % i, [128, L], BF16) for i in range(2)]
    lqT = [Tile("lq%d" % i) for i in range(2)]
    lkT = [Tile("lk%d" % i) for i in range(2)]
    Fb = [sb("f%d" % i, [128, 512], F32) for i in range(6)]
    FT = [Tile("f%d" % i) for i in range(6)]
    Bb = [sb("b%d" % i, [128, 512], BF16) for i in range(8)]
    BT = [Tile("b%d" % i) for i in range(8)]
    junk = sb("junk", [128, 1024], BF16)
    junkT = Tile("junk")
    Fj = junk.bitcast(F32)
    hnb = [sb("hnb%d" % i, [128, 1024], BF16) for i in range(2)]
    hnbT = [Tile("hnb%d" % i) for i in range(2)]
    Sst = sb("Sst", [128, 2, 512], F32)
    SstT = Tile("Sst")
    Sbf = sb("Sbf", [128, 2, 512], BF16)
    SbfT = Tile("Sbf")
    NST = 12
    stt_ = sb("stats", [128, NST, 8], F32)
    stT = [Tile("st%d" % i) for i in range(NST)]
    ssq = sb("ssq", [128, 2 * NB], F32)
    ssqT = [Tile("ssq%d" % i) for i in range(2 * NB)]
    ident = sb("identb", [128, 128], BF16)
    ones = sb("onesb", [128, 128], BF16)
    valid0 = sb("valid0b", [128, 128], BF16)
    MTp = sb("MTp_sb", [128, 4, 128], F32)
    ccols = sb("ccols_sb", [128, 16], F32)
    gfm = sb("gfm_sb", [128, G_TOT], F32)
    constT = Tile("consts")
    ps = es.enter_context(nc.psum_tensor("ps", [128, 8, 512], F32))
    psb = ps.bitcast(BF16)
    pT = [Tile("ps%d" % i) for i in range(8)]
    pT0a, pT0b, pT0c = Tile("ps0a"), Tile("ps0b"), Tile("ps0c")

    PA = RPool([0, 1])
    PA3 = RPool([0, 1])
    PS3 = RPool([2, 3, 6, 7])
    PB = RPool([2, 3])
    PC = RPool([4, 5])
    PD = RPool([6, 7])
    PDD = RPool([(4, 5), (6, 7)])
    Ball = RPool(list(range(8)))
    Brot = RPool([4, 5, 6, 7])
    hnbH = [Tile("hnbh%d" % i) for i in range(4)]
    PTB = RPool([(Bb[4], BT[4]), (hnb[0][:, 0:512], hnbH[0]), (Bb[5], BT[5]), (hnb[0][:, 512:1024], hnbH[1]),
                 (hnb[1][:, 0:512], hnbH[2]), (hnb[1][:, 512:1024], hnbH[3])])

    def alias_in(news, olds):
        for n_ in news:
            n_.w = None
            n_.rd = []
            for o_ in olds:
                n_.rd = n_.rd + list(o_.rd) + ([o_.w] if o_.w is not None else [])

    def alias_out(news, olds):
        for o_ in olds:
            for n_ in news:
                o_.rd = o_.rd + list(n_.rd) + ([n_.w] if n_.w is not None else [])

    Frot = RPool([2, 3, 4, 5])
    HNB = RPool([0, 1])
    ST = RPool(list(range(NST)))
    ring_ctr = [0]

    def ring_alloc():
        s = ring_ctr[0] % nring[0]
        ring_ctr[0] += 1
        return s

    def blk(t):
        return slice(t * 128, (t + 1) * 128)

    def mm(o, lhsT, rhs, start, stop, reads, writes):
        S.op("pe", lambda e: e.matmul(o, lhsT=lhsT, rhs=rhs, start=start, stop=stop), reads, writes)

    def tr(o, in_, reads, writes):
        S.op("pe", lambda e: e.transpose(out=o, in_=in_, identity=ident[:]), list(reads) + [constT], writes)

    def act(o, in_, func, reads, writes, scale=None, bias=None, accum=None):
        kw = {}
        if scale is not None:
            kw["scale"] = scale
        if bias is not None:
            kw["bias"] = bias
        if accum is not None:
            kw["accum_out"] = accum
        S.op("act", lambda e: e.activation(out=o, in_=in_, func=func, **kw), reads, writes)

    def tt(eng, o, a, b, op, reads, writes):
        S.op(eng, lambda e: e.tensor_tensor(out=o, in0=a, in1=b, op=op), reads, writes)

    def ts(eng, o, a, s1, s2, op0, op1, reads, writes):
        if s2 is None:
            S.op(eng, lambda e: e.tensor_scalar(out=o, in0=a, scalar1=s1, scalar2=None, op0=op0), reads, writes)
        else:
            S.op(eng, lambda e: e.tensor_scalar(out=o, in0=a, scalar1=s1, scalar2=s2, op0=op0, op1=op1), reads, writes)

    def stt(eng, o, a, sc, b, op0, op1, reads, writes):
        S.op(eng, lambda e: e.scalar_tensor_tensor(out=o, in0=a, scalar=sc, in1=b, op0=op0, op1=op1), reads, writes)

    def dma(eng, pairs, reads, writes, dsem):
        def fn(e, s):
            for (o, i) in pairs:
                e.dma_start(out=o, in_=i).then_inc(s, 16)
        S.op(eng, fn, reads, writes, dma=len(pairs), dsem=dsem)

    def rsqrt_col(o, in_, reads, writes):
        tt("pool", o, in_, ccols[:, 8:9], ALU.pow, list(reads) + [constT], writes)

    dma("pool", [(ident[:], ident_d), (ones[:], ones_d), (valid0[:], valid0_d)], [], [constT], "c0")
    dma("sp", [(MTp[:], MTp_d.rearrange("p (h i) -> p h i", h=4)), (ccols[:], ccols_d), (gfm[:], gfm_d)], [], [constT], "c1")

    def seq_begin(b):
        S.op("pool", lambda e: e.memset(h[:, 0, :], 0.0), [], [hT[0]])
        dma("sp", [(h[PAD:128, 0, :], meta)], [], [hT[0]], "hin4")
        for c in range(4):
            dma("sp", [(h[:, 1 + 4 * c:5 + 4 * c, :],
                        x[b, c * 512:(c + 1) * 512, :].rearrange("(t p) d -> p t d", p=128))],
                [], [hT[1 + 4 * c + i] for i in range(4)], "hin%d" % c)

    def norm_phase(gcol):
        banks = {}

        def s1(t):
            act(junk[:], h[:, t, :], AF.Square, [hT[t]], [ssqT[t], junkT], accum=ssq[:, t:t + 1])
            ts("dve", ssq[:, t:t + 1], ssq[:, t:t + 1], 1.0 / D, RMS_EPS, ALU.mult, ALU.add, [ssqT[t]], [ssqT[t]])
            rsqrt_col(ssq[:, NB + t:NB + t + 1], ssq[:, t:t + 1], [ssqT[t]], [ssqT[NB + t]])

        def s2(t):
            hbi = HNB.get()
            hb = hnb[hbi]
            act(hb[:], h[:, t, :], AF.Identity, [hT[t], ssqT[NB + t]], [hnbT[hbi]], scale=ssq[:, NB + t:NB + t + 1])
            bk = PA.get()
            pv = psb[:, bk, :].rearrange("p (k n) -> p k n", k=8)
            for k in range(8):
                tr(pv[:, k, :], hb[:, k * 128:(k + 1) * 128], [hnbT[hbi]], [pT[bk]])
            banks[t] = (bk, pv)

        def s3(t):
            bk, pv = banks[t]
            tt("dve", hnT[:, :, blk(t)], pv, gfm[:, gcol:gcol + 8].unsqueeze(2).broadcast_to([128, 8, 128]),
               ALU.mult, [pT[bk], constT], [hnTT[t]])

        s1(0)
        s1(1)
        for t in range(NB):
            if t + 2 < NB:
                s1(t + 2)
            s2(t)
            if t > 0:
                s3(t - 1)
        s3(NB - 1)

    def ffn(l, norm=None):
        npair = HID // 256
        look = max(1, nring[0] // 2 - 1)

        def load(p):
            sA = ring_alloc()
            sB = ring_alloc()
            w1v = rv(sA)[:, 0:2048].rearrange("p (k n) -> p k n", k=8)
            w3v = rv(sA)[:, 2048:4096].rearrange("p (k n) -> p k n", k=8)
            w2v = rv(sB)[:, 0:2048].rearrange("p (c n) -> p c n", c=2)
            dma("pool", [(w1v, ffn_w1[l, :, p * 256:(p + 1) * 256].rearrange("(k p) n -> p k n", p=128)),
                         (w3v, ffn_w3[l, :, p * 256:(p + 1) * 256].rearrange("(k p) n -> p k n", p=128))],
                [], ringT[sA], "ring%d" % sA)
            dma("pool", [(w2v, ffn_w2[l, p * 256:(p + 1) * 256, :].rearrange("(c p) n -> p c n", p=128))],
                [], ringT[sB], "ring%d" % sB)
            return (sA, sB, w1v, w3v, w2v)

        def compute(u):
            sA, sB, w1v, w3v, w2v = u
            pend = None

            def down(b0, nb, ajs):
                for tb in range(nb):
                    t = b0 + tb
                    dd = PDD.get()
                    for half in range(2):
                        for j in range(2):
                            mm(ps[:, dd[half], :], Bb[ajs[j]][:, blk(tb)], w2v[:, j, half * 512:(half + 1) * 512],
                               j == 0, j == 1, [BT[ajs[j]]] + ringT[sB], [pT[dd[half]]])
                    tt("dve", h[:, t, :].rearrange("p (a n) -> p a n", a=2),
                       h[:, t, :].rearrange("p (a n) -> p a n", a=2), ps[:, dd[0]:dd[0] + 2, :], ALU.add,
                       [hT[t], pT[dd[0]], pT[dd[1]]], [hT[t]])

            for (b0, nb) in TGS:
                N = nb * 128
                c0 = PAD if b0 == 0 else 0
                cols = slice(b0 * 128 + c0, b0 * 128 + N)
                hr = [hnTT[b0 + i] for i in range(nb)]
                ajs = []
                for j in range(2):
                    bg = PA.get()
                    bu = PB.get()
                    for k in range(8):
                        mm(ps[:, bg, c0:N], w1v[:, k, j * 128:(j + 1) * 128], hnT[:, k, cols], k == 0, k == 7,
                           ringT[sA] + hr, [pT[bg]])
                    for k in range(8):
                        mm(ps[:, bu, c0:N], w3v[:, k, j * 128:(j + 1) * 128], hnT[:, k, cols], k == 0, k == 7,
                           ringT[sA] + hr, [pT[bu]])
                    fi = Frot.get()
                    act(Fb[fi][:, c0:N], ps[:, bg, c0:N], AF.Silu, [pT[bg]], [FT[fi]])
                    bi = Ball.get()
                    if c0:
                        S.op("pool", lambda e, bi=bi, c0=c0: e.memset(Bb[bi][:, 0:c0], 0.0), [], [BT[bi]])
                    tt("dve", Bb[bi][:, c0:N], ps[:, bu, c0:N], Fb[fi][:, c0:N], ALU.mult, [pT[bu], FT[fi]], [BT[bi]])
                    ajs.append(bi)
                if pend is not None:
                    down(*pend)
                pend = (b0, nb, ajs)
            down(*pend)

        us = {}
        for p in range(min(look, npair)):
            us[p] = load(p)
        if norm is not None:
            norm()
        for p in range(npair):
            if p + look < npair:
                us[p + look] = load(p + look)
            compute(us.pop(p))

    def retention(l, norm=None):
        gam = gam128
        scT_ = kdT_ = ytT_ = pT[0]

        def load(hd, which):
            res = {}
            if "qk" in which:
                s = ring_alloc()
                v = rv(s).rearrange("p (k n) -> p k n", k=8)
                dma("pool", [(v[:, :, 0:256], ret_w_in[l, :, hd * 256:(hd + 1) * 256].rearrange("(k p) n -> p k n", p=128)),
                             (v[:, :, 256:512], ret_w_in[l, :, 1024 + hd * 256:1024 + (hd + 1) * 256].rearrange("(k p) n -> p k n", p=128))],
                    [], ringT[s], "ring%d" % s)
                res["qk"] = (s, v)
            if "v" in which:
                s = ring_alloc()
                v = rv(s).rearrange("p (k n) -> p k n", k=8)
                dma("pool", [(v, ret_w_in[l, :, 2048 + hd * 512:2048 + (hd + 1) * 512].rearrange("(k p) n -> p k n", p=128))],
                    [], ringT[s], "ring%d" % s)
                res["v"] = (s, v)
            if "g" in which:
                s = ring_alloc()
                v = rv(s).rearrange("p (k n) -> p k n", k=8)
                dma("pool", [(v, ret_w_in[l, :, 4096 + hd * 512:4096 + (hd + 1) * 512].rearrange("(k p) n -> p k n", p=128))],
                    [], ringT[s], "ring%d" % s)
                res["g"] = (s, v)
            if "o" in which:
                s = ring_alloc()
                v = rv(s).rearrange("p (c n) -> p c n", c=4)
                dma("pool", [(v, ret_w_o[l, hd * 512:(hd + 1) * 512, :].rearrange("(c p) n -> p c n", p=128))],
                    [], ringT[s], "ring%d" % s)
                res["o"] = (s, v)
            return res

        onv = [hnb[i].bitcast(F32) for i in range(2)]
        qTs = [longs[0][:, i * 1024:(i + 1) * 1024].rearrange("p (a n) -> p a n", a=2) for i in range(2)]
        kTs = [longs[1][:, i * 1024:(i + 1) * 1024].rearrange("p (a n) -> p a n", a=2) for i in range(2)]
        qTT = [lqT[0], lqT[1]]
        kTT = [lkT[0], lkT[1]]
        Ws = {}
        st = {}

        def qkproj_pe(hd, g):
            sQK, wQK = Ws[hd]["qk"]
            (b0, nb) = TGS[g]
            N = nb * 128
            cols = slice(b0 * 128, b0 * 128 + N)
            hr = [hnTT[b0 + i] for i in range(nb)]
            dma("sp", [(Fb[0][:, 0:N], ret_cosT[:, cols])], [], [FT[0]], "f0")
            dma("sp", [(Fb[1][:, 0:N], ret_sinT[:, cols])], [], [FT[1]], "f1")
            coff = 0
            for k in range(8):
                mm(ps[:, 1, 0:N], wQK[:, k, coff:coff + 128], hnT[:, k, cols], k == 0, k == 7, ringT[sQK] + hr, [pT[1]])
            act(Fb[2][:, 0:N], ps[:, 1, 0:N], AF.Copy, [pT[1]], [FT[2]])
            for k in range(8):
                mm(ps[:, 2, 0:N], wQK[:, k, coff + 128:coff + 256], hnT[:, k, cols], k == 0, k == 7, ringT[sQK] + hr, [pT[2]])
            act(Fb[3][:, 0:N], ps[:, 2, 0:N], AF.Copy, [pT[2]], [FT[3]])

        def qkproj_pe2(hd, g):
            sQK, wQK = Ws[hd]["qk"]
            (b0, nb) = TGS[g]
            N = nb * 128
            cols = slice(b0 * 128, b0 * 128 + N)
            hr = [hnTT[b0 + i] for i in range(nb)]
            coff = 256
            for k in range(8):
                mm(ps[:, 1, 0:N], wQK[:, k, coff:coff + 128], hnT[:, k, cols], k == 0, k == 7, ringT[sQK] + hr, [pT[1]])
            act(Fb[2][:, 0:N], ps[:, 1, 0:N], AF.Copy, [pT[1]], [FT[2]])
            for k in range(8):
                mm(ps[:, 2, 0:N], wQK[:, k, coff + 128:coff + 256], hnT[:, k, cols], k == 0, k == 7, ringT[sQK] + hr, [pT[2]])
            act(Fb[3][:, 0:N], ps[:, 2, 0:N], AF.Copy, [pT[2]], [FT[3]])

        def qkproj_ew(hd, g, which):
            (b0, nb) = TGS[g]
            N = nb * 128
            bi_ = (hd * len(TGS) + g) % 2
            dst, dT = (qTs[bi_], qTT[bi_]) if which == 0 else (kTs[bi_], kTT[bi_])
            tt("pool", Fb[4][:, 0:N], Fb[2][:, 0:N], Fb[0][:, 0:N], ALU.mult, [FT[2], FT[0]], [FT[4]])
            tt("pool", Fb[5][:, 0:N], Fb[3][:, 0:N], Fb[1][:, 0:N], ALU.mult, [FT[3], FT[1]], [FT[5]])
            tt("pool", dst[:, 0, 0:N], Fb[4][:, 0:N], Fb[5][:, 0:N], ALU.subtract, [FT[4], FT[5]], [dT])
            tt("dve", Fj[:, 0:N], Fb[2][:, 0:N], Fb[1][:, 0:N], ALU.mult, [FT[2], FT[1]], [junkT])
            tt("dve", Fb[2][:, 0:N], Fb[3][:, 0:N], Fb[0][:, 0:N], ALU.mult, [FT[3], FT[0]], [FT[2]])
            tt("dve", dst[:, 1, 0:N], Fj[:, 0:N], Fb[2][:, 0:N], ALU.add, [junkT, FT[2]], [dT])

        def qkproj(hd, g):
            qkproj_pe(hd, g)
            qkproj_ew(hd, g, 0)
            qkproj_pe2(hd, g)
            qkproj_ew(hd, g, 1)

        def P1a(key):
            hd, t, g, tb = key
            bi_ = (hd * len(TGS) + g) % 2
            bc = blk(tb)
            qT, kT = qTs[bi_], kTs[bi_]
            for dt in range(2):
                mm(ps[:, 0, 0:128], kT[:, dt, bc], qT[:, dt, bc], dt == 0, dt == 1, [qTT[bi_], kTT[bi_]], [pT[0]])
            si = Ball.get()
            tt("dve", Bb[si][:, 0:128], ps[:, 0, 0:128], MTp[:, hd, :], ALU.mult, [pT[0], constT], [BT[si]])
            st[key] = dict(si=si, qT=qT, bi=bi_, bc=bc)

        def P1b(key):
            hd, t, g, tb = key
            W = Ws[hd]
            sV, wV = W["v"]
            sG, wG = W["g"]
            d = st[key]
            bi_, bc = d["bi"], d["bc"]
            kT = kTs[bi_]
            for k in range(8):
                mm(ps[:, 1, :], hnT[:, k, blk(t)], wV[:, k, :], k == 0, k == 7, [hnTT[t]] + ringT[sV], [pT[1]])
            vi = Ball.get()
            act(Bb[vi][:], ps[:, 1, :], AF.Copy, [pT[1]], [BT[vi]])
            pk = psb[:, 0, 0:256].rearrange("p (a n) -> p a n", a=2)
            for dt in range(2):
                tr(pk[:, dt, :], kT[:, dt, bc], [kTT[bi_]], [pT[0]])
            ki = Ball.get()
            act(Bb[ki][:, 0:256], psb[:, 0, 0:256], AF.Identity, [pT[0], constT], [BT[ki]], scale=ccols[:, hd:hd + 1])
            for k in range(8):
                mm(ps[:, 2, :], hnT[:, k, blk(t)], wG[:, k, :], k == 0, k == 7, [hnTT[t]] + ringT[sG], [pT[2]])
            gi = Ball.get()
            act(Bb[gi][:], ps[:, 2, :], AF.Silu, [pT[2]], [BT[gi]])
            d.update(ki=ki, vi=vi, gi=gi)

        def P2(key):
            hd, t, g, tb = key
            d = st[key]
            ki, si, vi, qT, bi_, bc = d["ki"], d["si"], d["vi"], d["qT"], d["bi"], d["bc"]
            if t == 0:
                S.op("dve", lambda e: e.memset(Sst[:], 0.0), [], [SstT])
                S.op("pool", lambda e: e.memset(Sbf[:], 0.0), [], [SbfT])
            mm(ps[:, 3, :], Bb[si][:, 0:128], Bb[vi][:], True, False, [BT[si], BT[vi]], [pT[3]])
            for dt in range(2):
                mm(ps[:, 3, :], qT[:, dt, bc], Sbf[:, dt, :], False, dt == 1, [qTT[bi_], SbfT], [pT[3]])
            for dt in range(2):
                mm(ps[:, 4 + dt, :], Bb[ki][:, dt * 128:(dt + 1) * 128], Bb[vi][:], True, True,
                   [BT[ki], BT[vi]], [pT[4 + dt]])
            s1 = ST.get()
            sa = stt_[:, s1, :]
            S.op("dve", lambda e, sa=sa: e.bn_stats(out=sa[:, 0:6], in_=ps[:, 3, :]), [pT[3]], [stT[s1]])
            s2 = ST.get()
            sq = stt_[:, s2, :]
            S.op("dve", lambda e, sa=sa, sq=sq: e.bn_aggr(out=sq[:, 0:2], in_=sa[:, 0:6]), [stT[s1]], [stT[s2]])
            tt("dve", sq[:, 2:3], sq[:, 1:2], ccols[:, 4 + hd:5 + hd], ALU.add, [stT[s2], constT], [stT[s2]])
            rsqrt_col(sq[:, 3:4], sq[:, 2:3], [stT[s2]], [stT[s2]])
            stt("dve", Sst[:], Sst[:], gam[hd], ps[:, 4:6, :], ALU.mult, ALU.add,
                [SstT, pT[4], pT[5]], [SstT])
            act(Sbf[:], Sst[:], AF.Copy, [SstT], [SbfT])
            d.update(s2=s2, sq=sq)

        def GN(key):
            d = st[key]
            s2, sq, gi = d["s2"], d["sq"], d["gi"]
            stt("dve", sq[:, 4:5], sq[:, 0:1], -1.0, sq[:, 3:4], ALU.mult, ALU.mult, [stT[s2]], [stT[s2]])
            oi = HNB.get()
            act(onv[oi][:, :], ps[:, 3, :], AF.Identity, [pT[3], stT[s2]], [hnbT[oi]], scale=sq[:, 3:4], bias=sq[:, 4:5])
            yi = Ball.get()
            tt("dve", Bb[yi][:], onv[oi][:, :], Bb[gi][:], ALU.mult, [hnbT[oi], BT[gi]], [BT[yi]])
            d.update(yi=yi)

        def P3(key):
            hd = key[0]
            d = st[key]
            yi = d["yi"]
            py = psb[:, 2, 0:512].rearrange("p (c n) -> p c n", c=4)
            for c in range(4):
                tr(py[:, c, :], Bb[yi][:, c * 128:(c + 1) * 128], [BT[yi]], [pT[2]])
            yti = Ball.get()
            gc = G_GN + l * 16 + hd * 4
            tt("dve", Bb[yti][:].rearrange("p (c n) -> p c n", c=4), py,
               gfm[:, gc:gc + 4].unsqueeze(2).broadcast_to([128, 4, 128]), ALU.mult, [pT[2], constT], [BT[yti]])
            d.update(yti=yti)

        def P4(key):
            hd, t, g, tb = key
            sO, wO = Ws[hd]["o"]
            yti = st[key]["yti"]
            for half in range(2):
                for c in range(4):
                    mm(ps[:, 6 + half, :], Bb[yti][:, c * 128:(c + 1) * 128], wO[:, c, half * 512:(half + 1) * 512],
                       c == 0, c == 3, [BT[yti]] + ringT[sO], [pT[6 + half]])
            tt("dve", h[:, t, :].rearrange("p (a n) -> p a n", a=2),
               h[:, t, :].rearrange("p (a n) -> p a n", a=2), ps[:, 6:8, :], ALU.add,
               [hT[t], pT[6], pT[7]], [hT[t]])
            del st[key]

        items = []
        for hd in range(4):
            for g, (b0, nb) in enumerate(TGS):
                for tb in range(nb):
                    items.append((hd, b0 + tb, g, tb))
        Ws[0] = load(0, ("qk", "v", "g", "o"))
        Ws[1] = load(1, ("qk", "v", "g"))
        if norm is not None:
            norm()
        qkproj(0, 0)
        n = len(items)
        P1a(items[0])
        pending_k = None
        for i, key in enumerate(items):
            hd, t, g, tb = key
            nbk = TGS[g][1]
            nxt_qk = None
            if tb == max(nbk - 3, 0):
                if g + 1 < len(TGS):
                    nxt_qk = (hd, g + 1)
                elif hd + 1 < 4:
                    nxt_qk = (hd + 1, 0)
            do_k = pending_k
            pending_k = None
            if nxt_qk is not None:
                qkproj_pe(*nxt_qk)
            elif do_k is not None:
                qkproj_pe2(*do_k)
            P1b(key)
            P2(key)
            if i >= 1:
                P3(items[i - 1])
            if i >= 2:
                k2 = items[i - 2]
                P4(k2)
                if k2[1] == NB - 1 and k2[0] + 1 < 4:
                    Ws[k2[0] + 1].update(load(k2[0] + 1, ("o",)))
                    if k2[0] + 2 < 4:
                        Ws[k2[0] + 2] = load(k2[0] + 2, ("qk", "v", "g"))
            early = nxt_qk is not None and nbk == 1
            if early:
                qkproj_ew(nxt_qk[0], nxt_qk[1], 0)
                qkproj_pe2(*nxt_qk)
                qkproj_ew(nxt_qk[0], nxt_qk[1], 1)
            if i + 1 < n:
                P1a(items[i + 1])
            GN(key)
            if nxt_qk is not None and not early:
                qkproj_ew(nxt_qk[0], nxt_qk[1], 0)
                pending_k = nxt_qk
            elif do_k is not None:
                qkproj_ew(do_k[0], do_k[1], 1)
        P3(items[n - 1])
        P4(items[n - 2])
        P4(items[n - 1])

    def kv_phase():
        sA = ring_alloc()
        sB = ring_alloc()
        wA = rv(sA).rearrange("p (k n) -> p k n", k=8)
        wB = rv(sB)[:, 0:512].rearrange("p (k n) -> p k n", k=8)
        dma("pool", [(wA, w_kv_a[:, 0:512].rearrange("(k p) n -> p k n", p=128))], [], ringT[sA], "ring%d" % sA)
        dma("pool", [(wB, w_kv_a[:, 512:576].rearrange("(k p) n -> p k n", p=128))], [], ringT[sB], "ring%d" % sB)
        ktab = longs[0].bitcast(F32)
        dma("sp", [(ktab[:, :], mla_ktab)], [], [lqT[0], lqT[1]], "long0")
        ktv = ktab[:, :].rearrange("p (t c) -> p t c", c=64)
        norm_phase(G_KV)

        def kmm(t):
            dd = PDD.get()
            for k in range(8):
                mm(ps[:, dd[0], :], hnT[:, k, blk(t)], wA[:, k, :], k == 0, k == 7, [hnTT[t]] + ringT[sA], [pT[dd[0]]])
            for k in range(8):
                mm(ps[:, dd[1], 0:64], hnT[:, k, blk(t)], wB[:, k, :], k == 0, k == 7, [hnTT[t]] + ringT[sB], [pT[dd[1]]])
            return dd

        def krest(t, dd):
            s1 = ST.get()
            st = stt_[:, s1, :]
            act(junk[:, 0:512], ps[:, dd[0], :], AF.Square, [pT[dd[0]]], [stT[s1], junkT], accum=st[:, 0:1])
            ts("dve", st[:, 1:2], st[:, 0:1], 1.0 / 512, RMS_EPS, ALU.mult, ALU.add, [stT[s1]], [stT[s1]])
            rsqrt_col(st[:, 2:3], st[:, 1:2], [stT[s1]], [stT[s1]])
            ci = Ball.get()
            act(Bb[ci][:], ps[:, dd[0], :], AF.Identity, [pT[dd[0]], stT[s1]], [BT[ci]], scale=st[:, 2:3])
            bk = PA.get()
            pc = psb[:, bk, 0:512].rearrange("p (c n) -> p c n", c=4)
            for c in range(4):
                tr(pc[:, c, :], Bb[ci][:, c * 128:(c + 1) * 128], [BT[ci]], [pT[bk]])
            tt("dve", ckvT[:, :, blk(t)], pc, gfm[:, G_KVA:G_KVA + 4].unsqueeze(2).broadcast_to([128, 4, 128]),
               ALU.mult, [pT[bk], constT], [ckvTT[t]])
            fi = Frot.get()
            tmp = Fb[fi]
            x1 = ps[:, dd[1], 0:32]
            x2 = ps[:, dd[1], 32:64]
            cc_ = ktv[:, t, 0:32]
            ss_ = ktv[:, t, 32:64]
            rd = [pT[dd[1]], lqT[0], lqT[1]]
            tt("dve", tmp[:, 0:32], x1, cc_, ALU.mult, rd, [FT[fi]])
            tt("dve", tmp[:, 32:64], x2, ss_, ALU.mult, rd, [FT[fi]])
            tt("dve", tmp[:, 64:96], x1, ss_, ALU.mult, rd, [FT[fi]])
            tt("dve", tmp[:, 96:128], x2, cc_, ALU.mult, rd, [FT[fi]])
            ki = Ball.get()
            tt("pool", Bb[ki][:, 0:32], tmp[:, 0:32], tmp[:, 32:64], ALU.subtract, [FT[fi]], [BT[ki]])
            tt("pool", Bb[ki][:, 32:64], tmp[:, 64:96], tmp[:, 96:128], ALU.add, [FT[fi]], [BT[ki]])
            tt("pool", Bb[ki][:, 64:96], tmp[:, 0:32], tmp[:, 32:64], ALU.subtract, [FT[fi]], [BT[ki]])
            tt("pool", Bb[ki][:, 96:128], tmp[:, 64:96], tmp[:, 96:128], ALU.add, [FT[fi]], [BT[ki]])
            bk2 = PA.get()
            tr(psb[:, bk2, 0:128], Bb[ki][:, 0:128], [BT[ki]], [pT[bk2]])
            act(kropeT[:, blk(t)], psb[:, bk2, 0:128], AF.Copy, [pT[bk2]], [krTT[t]])

        dds = {0: kmm(0)}
        for t in range(NB):
            if t + 1 < NB:
                dds[t + 1] = kmm(t + 1)
            krest(t, dds.pop(t))

    def mla(j):
        l = 2 + j
        sA = ring_alloc()
        sB = ring_alloc()
        wA = rv(sA).rearrange("p (k n) -> p k n", k=8)
        wB = rv(sB)[:, 0:2048].rearrange("p (k n) -> p k n", k=8)
        dma("pool", [(wA, w_q_a[j, :, 0:512].rearrange("(k p) n -> p k n", p=128))], [], ringT[sA], "ring%d" % sA)
        dma("pool", [(wB, w_q_a[j, :, 512:768].rearrange("(k p) n -> p k n", p=128))], [], ringT[sB], "ring%d" % sB)
        norm_phase(G_MIX + l * 8)

        def qmm(t):
            dd = PDD.get()
            for k in range(8):
                mm(ps[:, dd[0], :], hnT[:, k, blk(t)], wA[:, k, :], k == 0, k == 7, [hnTT[t]] + ringT[sA], [pT[dd[0]]])
            for k in range(8):
                mm(ps[:, dd[1], 0:256], hnT[:, k, blk(t)], wB[:, k, :], k == 0, k == 7, [hnTT[t]] + ringT[sB], [pT[dd[1]]])
            return dd

        def qrest(t, dd):
            s1 = ST.get()
            st = stt_[:, s1, :]
            act(junk[:, 0:512], ps[:, dd[0], :], AF.Square, [pT[dd[0]]], [stT[s1], junkT], accum=st[:, 0:1])
            act(junk[:, 512:768], ps[:, dd[1], 0:256], AF.Square, [pT[dd[1]]], [stT[s1], junkT], accum=st[:, 1:2])
            tt("dve", st[:, 2:3], st[:, 0:1], st[:, 1:2], ALU.add, [stT[s1]], [stT[s1]])
            ts("dve", st[:, 3:4], st[:, 2:3], 1.0 / 768, RMS_EPS, ALU.mult, ALU.add, [stT[s1]], [stT[s1]])
            rsqrt_col(st[:, 4:5], st[:, 3:4], [stT[s1]], [stT[s1]])
            hbi = HNB.get()
            hb = hnb[hbi]
            act(hb[:, 0:512], ps[:, dd[0], :], AF.Identity, [pT[dd[0]], stT[s1]], [hnbT[hbi]], scale=st[:, 4:5])
            ts("dve", hb[:, 512:768], ps[:, dd[1], 0:256], st[:, 4:5], None, ALU.mult, None, [pT[dd[1]], stT[s1]], [hnbT[hbi]])
            bk = PA.get()
            pc = psb[:, bk, 0:768].rearrange("p (c n) -> p c n", c=6)
            for c in range(6):
                tr(pc[:, c, :], hb[:, c * 128:(c + 1) * 128], [hnbT[hbi]], [pT[bk]])
            gc = G_QA + j * 6
            tt("dve", hnT[:, 0:6, blk(t)], pc, gfm[:, gc:gc + 6].unsqueeze(2).broadcast_to([128, 6, 128]),
               ALU.mult, [pT[bk], constT], [hnTT[t]])

        dds = {0: qmm(0)}
        for t in range(NB):
            if t + 1 < NB:
                dds[t + 1] = qmm(t + 1)
            qrest(t, dds.pop(t))

        def load(hh):
            s = ring_alloc()
            wqb = rv(s)[:, 0:1536].rearrange("p (k n) -> p k n", k=6)
            wkvb = rv(s)[:, 1536:2560].rearrange("p (k n) -> p k n", k=4)
            wo = rv(s)[:, 2560:3584]
            c0 = hh * 192
            dma("pool", [(wqb[:, :, 0:192], w_q_b[j, :, c0:c0 + 192].rearrange("(k p) n -> p k n", p=128)),
                         (wqb[:, :, 192:224], w_q_b[j, :, c0 + 160:c0 + 192].rearrange("(k p) n -> p k n", p=128)),
                         (wqb[:, :, 224:256], w_q_b[j, :, c0 + 128:c0 + 160].rearrange("(k p) n -> p k n", p=128)),
                         (wkvb, w_kv_b[:, hh * 256:(hh + 1) * 256].rearrange("(k p) n -> p k n", p=128)),
                         (wo, mla_w_o[j, hh * 128:(hh + 1) * 128, :])],
                [], ringT[s], "ring%d" % s)
            return (s, wqb, wkvb, wo)

        LOOK = 3

        def compute(hh, W):
            s, wqb, wkvb, wo = W
            knT = longs[0]
            vh = longs[1][:, :].rearrange("p (t n) -> p t n", n=128)
            knTT = [lqT[0], lqT[1]]
            vhT = [lkT[0], lkT[1]]
            rT = ringT[s]

            def kvproj():
              for (b0, nb) in TGS:
                N = nb * 128
                cols = slice(b0 * 128, b0 * 128 + N)
                cr = [ckvTT[b0 + i] for i in range(nb)]
                bk = PA3.get()
                for kt in range(4):
                    mm(ps[:, bk, 0:N], wkvb[:, kt, 0:128], ckvT[:, kt, cols], kt == 0, kt == 3, rT + cr, [pT[bk]])
                act(knT[:, cols], ps[:, bk, 0:N], AF.Copy, [pT[bk]], knTT)
                bv = PA3.get()
                pvv = ps[:, bv, :].rearrange("p (t n) -> p t n", n=128)
                for tb in range(nb):
                    for kt in range(4):
                        mm(pvv[:, tb, :], ckvT[:, kt, blk(b0 + tb)], wkvb[:, kt, 128:256], kt == 0, kt == 3,
                           rT + [ckvTT[b0 + tb]], [pT[bv]])
                act(vh[:, b0:b0 + nb, :], pvv[:, 0:nb, :], AF.Copy, [pT[bv]], vhT)
            Fr2 = RPool([2, 3])
            state = {}

            def stageA(g):
                (b0, nb) = TGS[g]
                N = nb * 128
                cols = slice(b0 * 128, b0 * 128 + N)
                hr = [hnTT[b0 + i] for i in range(nb)]
                dma("sp", [(Fb[0][:, 0:N], mla_CS[:, cols])], [], [FT[0]], "f0")
                qni = g % 2
                qri = 2 + g % 2
                bq = PA3.get()
                for kt in range(6):
                    mm(ps[:, bq, 0:N], wqb[:, kt, 0:128], hnT[:, kt, cols], kt == 0, kt == 5, rT + hr, [pT[bq]])
                act(Bb[qni][:, 0:N], ps[:, bq, 0:N], AF.Identity, [pT[bq]], [BT[qni]], scale=float(192.0 ** -0.5))
                ba = PA3.get()
                for kt in range(6):
                    mm(ps[:, ba, 0:N], wqb[:, kt, 128:256], hnT[:, kt, cols], kt == 0, kt == 5, rT + hr, [pT[ba]])
                tt("dve", Bb[qri][:, 0:N], ps[:, ba, 0:N], Fb[0][:, 0:N], ALU.mult, [pT[ba], FT[0]], [BT[qri]])

            def stageB(g, pend=None):
                pend = pend if pend is not None else []
                (b0, nb) = TGS[g]
                N = nb * 128
                qni = g % 2
                qri = 2 + g % 2
                last = b0 + nb - 1
                chunks = []
                for jb in range(0, last + 1):
                    i0 = max(jb, b0)
                    chunks.append((jb, (last - i0 + 1) * 128, (i0 - b0) * 128))

                def STc(c):
                    jb, ncol, q0 = c
                    bs = PS3.get()
                    mm(ps[:, bs, 0:ncol], knT[:, blk(jb)], Bb[qni][:, q0:q0 + ncol], True, False,
                       knTT + [BT[qni]], [pT[bs]])
                    mm(ps[:, bs, 0:ncol], kropeT[:, blk(jb)], Bb[qri][:, q0:q0 + ncol], False, True,
                       [krTT[jb], BT[qri]], [pT[bs]])
                    pi = PTB.get()
                    pb, pbT = pi
                    act(pb[:, 0:ncol], ps[:, bs, 0:ncol], AF.Exp, [pT[bs]], [pbT])
                    if jb >= b0:
                        act(pb[64:128, 0:64], ones[64:128, 0:64], AF.Copy, [constT], [pbT], scale=0.0)
                    return pi

                def PVc(c, pi):
                    jb, ncol, q0 = c
                    pb, pbT = pi
                    mm(ps[:, 4, q0:q0 + ncol], vh[:, jb, :], pb[:, 0:ncol], jb == 0, jb == last,
                       vhT + [pbT], [pT[4]])
                    vl = valid0 if jb == 0 else ones
                    mm(ps[:, 5, q0:q0 + ncol], vl[:], pb[:, 0:ncol], jb == 0, jb == last,
                       [constT, pbT], [pT[5]])

                n = len(chunks)
                pis = {}
                for k in range(min(LOOK, n)):
                    pis[k] = STc(chunks[k])
                for k in range(n):
                    if k + LOOK < n:
                        pis[k + LOOK] = STc(chunks[k + LOOK])
                    PVc(chunks[k], pis.pop(k))
                    if pend:
                        pend.pop(0)()
                while pend:
                    pend.pop(0)()

            def stageC1(g):
                (b0, nb) = TGS[g]
                N = nb * 128
                fr = 4 + g % 2
                S.op("dve", lambda e, fr=fr, N=N: e.reciprocal(out=Fb[fr][:, 0:N], in_=ps[:, 5, 0:N]), [pT[5]], [FT[fr]])
                oi = 6 + g % 2
                tt("dve", Bb[oi][:, 0:N], ps[:, 4, 0:N], Fb[fr][:, 0:N], ALU.mult, [pT[4], FT[fr]], [BT[oi]])

            def stageC2(g):
                (b0, nb) = TGS[g]
                oi = 6 + g % 2
                items = []
                for tb in range(nb):
                    for half in range(2):
                        def item(tb=tb, half=half):
                            t = b0 + tb
                            bw = PA3.get()
                            mm(ps[:, bw, :], Bb[oi][:, blk(tb)], wo[:, half * 512:(half + 1) * 512], True, True,
                               [BT[oi]] + rT, [pT[bw]])
                            hs = h[:, t, half * 512:(half + 1) * 512]
                            tt("dve", hs, hs, ps[:, bw, :], ALU.add, [hT[t], pT[bw]], [hT[t]])
                        items.append(item)
                return items

            return kvproj, stageA, stageB, stageC1, stageC2

        G = len(TGS)
        alias_in(hnbH, hnbT)
        Wd = {0: load(0), 1: load(1)}
        cur = compute(0, Wd[0])
        cur[0]()
        cur[1](0)
        cur[1](1)
        pend = []
        for hh in range(8):
            kvp, sA_, sB_, sC1_, sC2_ = cur
            if hh + 2 < 8:
                Wd[hh + 2] = load(hh + 2)
            nxt = None
            for g in range(G):
                sB_(g, pend)
                sC1_(g)
                if g + 2 < G:
                    sA_(g + 2)
                if g == G - 1 and hh + 1 < 8:
                    nxt = compute(hh + 1, Wd[hh + 1])
                    nxt[0]()
                    nxt[1](0)
                    nxt[1](1)
                pend = sC2_(g)
            cur = nxt
        while pend:
            pend.pop(0)()
        alias_out(hnbH, hnbT)

    def final_phase(b):
        dma("sp", [(Fb[0][:], final_g[:, 0:512].broadcast_to([128, 512]))], [], [FT[0]], "f0")
        dma("sp", [(Fb[1][:], final_g[:, 512:1024].broadcast_to([128, 512]))], [], [FT[1]], "f1")
        for t in range(1, NB):
            hbi = HNB.get()
            act(hnb[hbi][:], h[:, t, :], AF.Square, [hT[t]], [hnbT[hbi], ssqT[t]], accum=ssq[:, t:t + 1])
            ts("dve", ssq[:, t:t + 1], ssq[:, t:t + 1], 1.0 / D, RMS_EPS, ALU.mult, ALU.add, [ssqT[t]], [ssqT[t]])
            rsqrt_col(ssq[:, NB + t:NB + t + 1], ssq[:, t:t + 1], [ssqT[t]], [ssqT[NB + t]])
            for half in range(2):
                hs = h[:, t, half * 512:(half + 1) * 512]
                stt("dve", hs, hs, ssq[:, NB + t:NB + t + 1], Fb[half][:], ALU.mult, ALU.mult,
                    [hT[t], ssqT[NB + t], FT[half]], [hT[t]])
        for c in range(4):
            dma("sp", [(out[b, c * 512:(c + 1) * 512, :].rearrange("(t p) d -> p t d", p=128), h[:, 1 + 4 * c:5 + 4 * c, :])],
                [hT[1 + 4 * c + i] for i in range(4)], [], "hout%d" % c)

    for b in range(nseq):
        seq_begin(b)
        nring[0] = 7
        step = 0
        for l in range(2):
            if step < stop_after:
                retention(l, norm=lambda l=l: norm_phase(G_MIX + l * 8))
            step += 1
            if step < stop_after:
                ffn(l, norm=lambda l=l: norm_phase(G_FFN + l * 8))
            step += 1
        nring[0] = 5
        if step < stop_after:
            kv_phase()
        step += 1
        for j in range(2):
            if step < stop_after:
                mla(j)
            step += 1
            if step < stop_after:
                ffn(2 + j, norm=lambda j=j: norm_phase(G_FFN + (2 + j) * 8))
            step += 1
        final_phase(b)

    S.emit(final_waits=["hout%d" % c for c in range(4)])
    es.close()
    return nc, S


_CACHE = {}


def kernel(x, meta, norm_mix_g, norm_ffn_g, ret_w_in, ret_gn_g, ret_w_o,
           mla_norm_kv_g, mla_w_kv_a, mla_kv_a_norm_g, mla_w_kv_b,
           mla_w_q_a, mla_q_a_norm_g, mla_w_q_b, mla_w_o,
           ffn_w1, ffn_w3, ffn_w2, final_g, _nseq=None, _stop_after=9, _ncores=None):
    f = lambda a: np.ascontiguousarray(np.asarray(a, dtype=np.float32))
    x = f(x)
    B = x.shape[0]
    ncores = _ncores or NCORES
    nseq = _nseq or (B // ncores)
    consts, gam128 = _consts()
    key = (nseq, _stop_after)
    if key not in _CACHE:
        _CACHE[key] = build_nc(nseq, _stop_after, gam128)[0]
    nc = _CACHE[key]
    gfm = np.zeros((128, G_TOT), np.float32)
    nm, nf = f(norm_mix_g), f(norm_ffn_g)
    for l in range(4):
        gfm[:, G_MIX + l * 8:G_MIX + (l + 1) * 8] = _fm(nm[l], 8)
        gfm[:, G_FFN + l * 8:G_FFN + (l + 1) * 8] = _fm(nf[l], 8)
    gfm[:, G_KV:G_KV + 8] = _fm(mla_norm_kv_g, 8)
    gn = f(ret_gn_g)
    for l in range(2):
        gfm[:, G_GN + l * 16:G_GN + (l + 1) * 16] = _fm(gn[l], 16)
    qa = f(mla_q_a_norm_g)
    for j in range(2):
        gfm[:, G_QA + j * 6:G_QA + (j + 1) * 6] = _fm(qa[j], 6)
    gfm[:, G_KVA:G_KVA + 4] = _fm(mla_kv_a_norm_g, 4)
    shared = {
        "meta": f(meta), "ret_w_in": f(ret_w_in), "ret_w_o": f(ret_w_o),
        "mla_w_kv_a": f(mla_w_kv_a), "mla_w_kv_b": f(mla_w_kv_b), "mla_w_q_a": f(mla_w_q_a),
        "mla_w_q_b": f(mla_w_q_b), "mla_w_o": f(mla_w_o), "ffn_w1": f(ffn_w1), "ffn_w3": f(ffn_w3),
        "ffn_w2": f(ffn_w2), "final_g": f(final_g).reshape(1, D), "gfm": gfm,
        "ret_cosT": consts["ret_cosT"], "ret_sinT": consts["ret_sinT"], "mla_CS": consts["mla_CS"], "mla_ktab": consts["mla_ktab"].reshape(128, NB * 64),
        "ident": consts["ident"], "ones": consts["ones"], "valid0": consts["valid0"],
        "MTp": consts["MTp"].reshape(128, 512), "ccols": consts["ccols"],
    }
    in_maps = []
    for c in range(ncores):
        m = dict(shared)
        m["x"] = np.ascontiguousarray(x[c * nseq:(c + 1) * nseq])
        in_maps.append(m)
    res = run_bass_kernel_spmd(nc, in_maps, core_ids=list(range(ncores)))
    return np.concatenate([np.asarray(r["out"], dtype=np.float32) for r in res.results], axis=0)
```
